# Optimizing a Trainium2 kernel written in Bass

```python
import math
import jax, jax.numpy as jnp
from jax import lax
import numpy as np

D_MODEL = 1024
BATCH = 2
SEQ = 8192
DEPTH = 1

MOBA_HEAD_DIM = 64
MOBA_HEADS = (D_MODEL // 2) // MOBA_HEAD_DIM
MOBA_WIDTH = MOBA_HEADS * MOBA_HEAD_DIM
MOBA_BLOCK = 256
MOBA_TOPK = 3
Q_BLOCK = 128
GLA_HEADS = 4
GLA_VAL_DIM = (D_MODEL // 2) // GLA_HEADS
GLA_KEY_DIM = GLA_VAL_DIM // 2
GLA_VWIDTH = GLA_HEADS * GLA_VAL_DIM
GLA_KWIDTH = GLA_HEADS * GLA_KEY_DIM
GLA_GATE_RANK = 16
GLA_GATE_TAU = 16.0
GLA_CHUNK = 64
MIX_WIDTH = MOBA_WIDTH + GLA_VWIDTH
IN_COLS = 3 * MOBA_WIDTH + 2 * GLA_KWIDTH + 2 * GLA_VWIDTH + GLA_GATE_RANK
D_FF = 256 * ((8 * D_MODEL // 3 + 255) // 256)
CONV_WIDTH = 3
ROPE_THETA = 10000.0
EPS = 1e-6

kernel_name = "hymba_moba_gla_convglu"


def rms_norm(x, g):
    xf = x.astype(jnp.float32)
    y = xf * lax.rsqrt(jnp.mean(xf * xf, axis=-1, keepdims=True) + EPS)
    return (y * g.astype(jnp.float32)).astype(x.dtype)


def rope(t, pos):
    hd = t.shape[-1]
    inv_freq = 1.0 / (ROPE_THETA ** (jnp.arange(0, hd, 2, dtype=jnp.float32) / hd))
    ang = pos.astype(jnp.float32)[:, None] * inv_freq[None, :]
    cos = jnp.cos(ang).astype(t.dtype)
    sin = jnp.sin(ang).astype(t.dtype)
    t1, t2 = t[..., : hd // 2], t[..., hd // 2:]
    return jnp.concatenate([t1 * cos - t2 * sin, t2 * cos + t1 * sin], axis=-1)


def moba_attention(q, k, v):
    B, H, S, hd = q.shape
    nb = -(-S // MOBA_BLOCK)
    n_sel = min(MOBA_TOPK, nb)
    pad = nb * MOBA_BLOCK - S
    padw = ((0, 0), (0, 0), (0, pad), (0, 0))
    kp = jnp.pad(k, padw).reshape(B, H, nb, MOBA_BLOCK, hd)
    vp = jnp.pad(v, padw).reshape(B, H, nb, MOBA_BLOCK, hd)
    k_mean = jnp.mean(kp, axis=3)
    q = q * (hd ** -0.5)
    n_qb = S // Q_BLOCK
    gather = jax.vmap(jax.vmap(lambda blocks, idx: blocks[idx]))
    block_ids = jnp.arange(nb)

    def one_query_block(c):
        q0 = c * Q_BLOCK
        qc = lax.dynamic_slice_in_dim(q, q0, Q_BLOCK, axis=2)
        own = q0 // MOBA_BLOCK
        qpos = q0 + jnp.arange(Q_BLOCK)
        gate = jnp.einsum('bhqd,bhnd->bhqn', qc, k_mean).astype(jnp.float32)
        gate = jnp.where((block_ids < own)[None, None, None, :], gate, -jnp.inf)
        top_s, top_i = lax.top_k(gate, n_sel)
        valid = top_s > -jnp.inf
        kg = gather(kp, top_i)
        vg = gather(vp, top_i)
        s_sel = jnp.einsum('bhqd,bhqnkd->bhqnk', qc, kg).astype(jnp.float32)
        s_sel = jnp.where(valid[..., None], s_sel, -jnp.inf)
        s_sel = s_sel.reshape(B, H, Q_BLOCK, n_sel * MOBA_BLOCK)
        k_own = lax.dynamic_index_in_dim(kp, own, axis=2, keepdims=False)
        v_own = lax.dynamic_index_in_dim(vp, own, axis=2, keepdims=False)
        kpos = own * MOBA_BLOCK + jnp.arange(MOBA_BLOCK)
        s_own = jnp.einsum('bhqd,bhkd->bhqk', qc, k_own).astype(jnp.float32)
        s_own = jnp.where(qpos[:, None] >= kpos[None, :], s_own, -jnp.inf)
        p = jax.nn.softmax(jnp.concatenate([s_sel, s_own], axis=-1), axis=-1).astype(v.dtype)
        p_sel = p[..., : n_sel * MOBA_BLOCK].reshape(B, H, Q_BLOCK, n_sel, MOBA_BLOCK)
        p_own = p[..., n_sel * MOBA_BLOCK:]
        return (jnp.einsum('bhqnk,bhqnkd->bhqd', p_sel, vg)
                + jnp.einsum('bhqk,bhkd->bhqd', p_own, v_own))

    out = lax.map(one_query_block, jnp.arange(n_qb))
    return out.transpose(1, 2, 0, 3, 4).reshape(B, H, S, hd)


def gla_attention(q, k, v, log_a):
    B, H, S, dk = q.shape
    dv = v.shape[-1]
    C = GLA_CHUNK
    nc = S // C
    f32 = jnp.float32

    def to_chunks(t):
        return t.astype(f32).reshape(B, H, nc, C, t.shape[-1]).transpose(2, 0, 1, 3, 4)

    qc = to_chunks(q * (dk ** -0.5))
    kc, vc = to_chunks(k), to_chunks(v)
    G = jnp.cumsum(to_chunks(log_a), axis=3)
    causal = jnp.tril(jnp.ones((C, C), dtype=bool))

    def step(state, inp):
        q_, k_, v_, G_ = inp
        diff = G_[:, :, :, None, :] - G_[:, :, None, :, :]
        decay = jnp.exp(jnp.where(causal[None, None, :, :, None], diff, -jnp.inf))
        A = jnp.einsum('bhtd,bhsd,bhtsd->bhts', q_, k_, decay)
        o = (jnp.einsum('bhts,bhsv->bhtv', A, v_)
             + jnp.einsum('bhtd,bhdv->bhtv', q_ * jnp.exp(G_), state))
        G_last = G_[:, :, -1:, :]
        state = (jnp.exp(G_last)[:, :, 0, :, None] * state
                 + jnp.einsum('bhsd,bhsv->bhdv', k_ * jnp.exp(G_last - G_), v_))
        return state, o

    state0 = jnp.zeros((B, H, dk, dv), f32)
    _, o = lax.scan(step, state0, (qc, kc, vc, G))
    return o.transpose(1, 2, 0, 3, 4).reshape(B, H, S, dv)


def causal_dwconv(h, w, b):
    S = h.shape[1]
    hp = jnp.pad(h, ((0, 0), (CONV_WIDTH - 1, 0), (0, 0)))
    return sum(w[j] * hp[:, j:j + S] for j in range(CONV_WIDTH)) + b


def split_heads(t, n_heads):
    B, S, W = t.shape
    return t.reshape(B, S, n_heads, W // n_heads).transpose(0, 2, 1, 3)


def merge_heads(t):
    B, H, S, d = t.shape
    return t.transpose(0, 2, 1, 3).reshape(B, S, H * d)


def setup_inputs(seed: int = 0) -> dict:
    key = jax.random.key(seed)
    ks = jax.random.split(key, 16)
    f32 = jnp.float32
    D, L = D_MODEL, DEPTH
    nrm = lambda k, shape, fan: jax.random.normal(k, shape, f32) * (fan ** -0.5)
    return {
        "x": jax.random.normal(ks[0], (BATCH, SEQ, D), f32),
        "attn_norm_g": 1.0 + 0.02 * jax.random.normal(ks[1], (L, D), f32),
        "w_in": nrm(ks[2], (L, D, IN_COLS), D),
        "w_gate_up": nrm(ks[3], (L, GLA_GATE_RANK, GLA_KWIDTH), GLA_GATE_RANK),
        "b_gate": 0.1 * jax.random.normal(ks[4], (L, GLA_KWIDTH), f32),
        "gla_norm_g": 1.0 + 0.02 * jax.random.normal(ks[5], (L, GLA_HEADS, GLA_VAL_DIM), f32),
        "w_out": nrm(ks[6], (L, MIX_WIDTH, D), MIX_WIDTH),
        "ffn_norm_g": 1.0 + 0.02 * jax.random.normal(ks[7], (L, D), f32),
        "w_ffn_up": nrm(ks[8], (L, D, 2 * D_FF), D),
        "conv_w": nrm(ks[9], (L, CONV_WIDTH, 2 * D_FF), CONV_WIDTH),
        "conv_b": 0.02 * jax.random.normal(ks[10], (L, 2 * D_FF), f32),
        "w_ffn_down": nrm(ks[11], (L, D_FF, D), D_FF),
        "final_norm_g": 1.0 + 0.02 * jax.random.normal(ks[12], (D,), f32),
    }


def reference(x, attn_norm_g, w_in, w_gate_up, b_gate, gla_norm_g, w_out,
              ffn_norm_g, w_ffn_up, conv_w, conv_b, w_ffn_down, final_norm_g):
    B, S, _ = x.shape
    pos = jnp.arange(S)
    o_mq = 0
    o_mk = o_mq + MOBA_WIDTH
    o_mv = o_mk + MOBA_WIDTH
    o_gq = o_mv + MOBA_WIDTH
    o_gk = o_gq + GLA_KWIDTH
    o_gv = o_gk + GLA_KWIDTH
    o_gr = o_gv + GLA_VWIDTH
    o_gg = o_gr + GLA_VWIDTH
    for l in range(DEPTH):
        xn = rms_norm(x, attn_norm_g[l])
        proj = xn @ w_in[l]
        mq = rope(split_heads(proj[..., o_mq:o_mk], MOBA_HEADS), pos)
        mk = rope(split_heads(proj[..., o_mk:o_mv], MOBA_HEADS), pos)
        mv = split_heads(proj[..., o_mv:o_gq], MOBA_HEADS)
        y_moba = merge_heads(moba_attention(mq, mk, mv))
        gq = split_heads(proj[..., o_gq:o_gk], GLA_HEADS)
        gk = split_heads(proj[..., o_gk:o_gv], GLA_HEADS)
        gv = split_heads(proj[..., o_gv:o_gr], GLA_HEADS)
        gr = proj[..., o_gr:o_gg]
        gate_lr = proj[..., o_gg:]
        log_a = jax.nn.log_sigmoid((gate_lr @ w_gate_up[l] + b_gate[l]).astype(jnp.float32)) / GLA_GATE_TAU
        og = gla_attention(gq, gk, gv, split_heads(log_a, GLA_HEADS))
        og = og * lax.rsqrt(jnp.mean(og * og, axis=-1, keepdims=True) + EPS)
        og = og * gla_norm_g[l].astype(jnp.float32)[None, :, None, :]
        y_gla = merge_heads(og).astype(x.dtype) * jax.nn.silu(gr)
        x = x + jnp.concatenate([y_moba, y_gla], axis=-1) @ w_out[l]
        hn = rms_norm(x, ffn_norm_g[l])
        h = causal_dwconv(hn @ w_ffn_up[l], conv_w[l], conv_b[l])
        h_gate, h_val = h[..., :D_FF], h[..., D_FF:]
        x = x + (jax.nn.silu(h_gate) * h_val) @ w_ffn_down[l]
    return rms_norm(x, final_norm_g)
```

```python
import contextlib
import os
import math

import numpy as np
import ml_dtypes

import concourse.bass as bass
import concourse.mybir as mybir
from concourse.bass_utils import run_bass_kernel_spmd

F32 = mybir.dt.float32
BF16 = mybir.dt.bfloat16
ALU = mybir.AluOpType
AF = mybir.ActivationFunctionType
AX = mybir.AxisListType

D = 1024
S = 8192
B = 2
DFF = 2816
NFC = DFF // 128
EPS = 1e-6
NEG = -30000.0


def _units(lst):
    out, cur = [], []
    for st in lst:
        cur.append(st)
        if not st[2].get("hold", False):
            out.append(cur)
            cur = []
    if cur:
        out.append(cur)
    return out


def merge_steps(*lists):
    us = [_units(l) for l in lists if l]
    if not us:
        return []
    tot = [len(u) for u in us]
    pos = [0] * len(us)
    out = []
    while any(p < t for p, t in zip(pos, tot)):
        k = min((i for i in range(len(us)) if pos[i] < tot[i]), key=lambda i: (pos[i] + 0.5) / tot[i])
        out.extend(us[k][pos[k]])
        pos[k] += 1
    return out


class Sched:
    COMPUTE = ("pe", "act", "dve", "pool")

    def __init__(self, nc, es, prefix=""):
        self.nc = nc
        self.es = es
        self.prefix = prefix
        self._cc = {}
        self.defer = None
        self.batch_no = 0
        self.tok_batch = {}
        self.streams = {e: [] for e in ("pe", "act", "dve", "pool", "sp")}
        self.sems = {}
        self.dma_counts = {}
        self.dma_inc = {}
        self.last_w = {}
        self.readers = {}
        self.nsem = 0

    def sem(self, key):
        if key not in self.sems:
            self.sems[key] = self.es.enter_context(self.nc.semaphore(self.prefix + "s%d_%s" % (self.nsem, str(key)[:20].replace(":", "_").replace(" ", ""))))
            self.nsem += 1
        return self.sems[key]

    def _deps(self, reads, writes, eng=None):
        deps = []
        for t in reads:
            if t in self.last_w:
                deps.append(self.last_w[t])
            if t.startswith("p"):
                for r in self.readers.get(t, ()):
                    if not (r[0] == "c" and r[1] == eng):
                        deps.append(r)
        for t in writes:
            if t in self.last_w:
                deps.append(self.last_w[t])
            deps.extend(self.readers.get(t, ()))
        return deps

    def _register(self, ref, reads, writes):
        for t in reads:
            self.readers.setdefault(t, []).append(ref)
        for t in writes:
            self.last_w[t] = ref
            self.readers[t] = []

    def run_deferred(self, lst, n, iters_left=None):
        saved, self.defer = self.defer, None
        done = 0
        hold = False
        if iters_left is None:
            lo = hi = n
        else:
            lo = -(-len(lst) // max(1, iters_left))
            hi = max(4, 2 * lo) if iters_left > 1 else len(lst)
        self.batch_no += 1
        DIST = int(os.environ.get('K_DIST', 1))
        while lst and (done < hi or hold):
            kind, a, kw = lst[0]
            if (iters_left is not None and iters_left > 1 and not hold and kind == "op" and a[0] in ("pe", "act")
                    and any(self.tok_batch.get(r, -99) > self.batch_no - DIST for r in kw["reads"])):
                break
            lst.pop(0)
            kw = dict(kw)
            hold = kw.pop("hold", False)
            (self.op if kind == "op" else self.dma)(*a, **kw)
            for w in kw["writes"]:
                self.tok_batch[w] = self.batch_no
            done += 1
            if iters_left is None and done >= n and not hold:
                break
        self.defer = saved

    def op(self, eng, fn, reads=(), writes=(), signal=True, hold=False):
        if self.defer is not None:
            self.defer.append(("op", (eng, fn), dict(reads=list(reads), writes=list(writes), signal=signal, hold=hold)))
            return
        deps = self._deps(reads, writes, eng)
        st = self.streams[eng]
        idx = len(st)
        st.append(dict(fn=fn, deps=deps, signal=signal, dma=None))
        self._register(("c", eng, idx), reads, writes)

    def dma(self, queue, slot, fn, reads=(), writes=(), inc=16):
        if self.defer is not None:
            self.defer.append(("dma", (queue, slot, fn), dict(reads=list(reads), writes=list(writes), inc=inc)))
            return
        deps = self._deps(reads, writes)
        n = self.dma_counts.get(slot, 0) + 1
        self.dma_counts[slot] = n
        self.dma_inc[slot] = inc
        st = self.streams[queue]
        st.append(dict(fn=fn, deps=deps, signal=False, dma=slot))
        self._register(("d", slot, inc * n), reads, writes)

    def wait_all(self, queue, tokens):
        deps = []
        for t in tokens:
            if t in self.last_w:
                deps.append(self.last_w[t])
        self.streams[queue].append(dict(fn=None, deps=deps, signal=False, dma=None))

    def core_c(self, eng):
        k = id(eng)
        if k not in self._cc:
            self._cc[k] = eng.partition_id() % 4
        return self._cc[k]

    def final_waits(self):
        out = []
        for e in self.COMPUTE:
            n = sum(1 for o in self.streams[e] if o["signal"])
            if n:
                out.append((self.sem(("eng", e)), n))
        for slot, n in self.dma_counts.items():
            out.append((self.sem(("dma", slot)), self.dma_inc[slot] * n))
        return out

    def emit(self, block, pre_waits=()):
        sigval = {}
        for e in self.COMPUTE:
            st = self.streams[e]
            vals = [0] * len(st)
            cnt = 0
            for i, o in enumerate(st):
                if o["signal"]:
                    cnt += 1
                vals[i] = cnt if o["signal"] else None
            nxt = None
            for i in range(len(st) - 1, -1, -1):
                if vals[i] is None:
                    assert nxt is not None or st[i]["dma"] is not None or st[i]["fn"] is None, "last op on %s must signal" % e
                    vals[i] = nxt
                else:
                    nxt = vals[i]
            sigval[e] = vals
        for e in self.COMPUTE:
            self.sem(("eng", e))
        for s in self.dma_counts:
            self.sem(("dma", s))

        def run(ename, eng):
            waited = {}
            for (psem, pval) in pre_waits:
                eng.wait_ge(psem, pval)
            for o in self.streams[ename]:
                need = {}
                for d in o["deps"]:
                    if d[0] == "c":
                        _, e2, idx = d
                        if e2 == ename and ename in ("pe",):
                            continue
                        if e2 == ename and ename == "sp":
                            continue
                        v = sigval[e2][idx]
                        if v is None:
                            continue
                        key = ("eng", e2)
                    else:
                        _, slot, v = d
                        key = ("dma", slot)
                    if v > need.get(key, 0):
                        need[key] = v
                for key, v in need.items():
                    if waited.get(key, 0) >= v:
                        continue
                    eng.wait_ge(self.sems[key], v)
                    waited[key] = v
                if o["fn"] is None:
                    continue
                ins = o["fn"](eng)
                if o["dma"] is not None:
                    ins.then_inc(self.sems[("dma", o["dma"])], self.dma_inc[o["dma"]])
                elif o["signal"]:
                    ins.then_inc(self.sems[("eng", ename)], 1)

        @block.tensor
        def _(eng):
            run("pe", eng)

        @block.scalar
        def _(eng):
            run("act", eng)

        @block.vector
        def _(eng):
            run("dve", eng)

        @block.gpsimd
        def _(eng):
            run("pool", eng)

        @block.sync
        def _(eng):
            run("sp", eng)


def bcast_rows(ap_1xn, nparts):
    return bass.AP(ap_1xn.tensor, ap_1xn.offset, [[0, nparts]] + [list(x) for x in ap_1xn.ap[1:]])


import os
NTOK2 = int(os.environ.get('K_NTOK2', int(os.environ.get('K_S1', 8192)) // 4))
HALO = 128
GW = 256
NG2 = NTOK2 // GW


def build_ffn(ctx=None):
    fused = ctx is not None
    nc = ctx["nc"] if fused else bass.Bass("TRN2", target_bir_lowering=False)
    TT = HALO + NTOK2
    if fused:
        x_in = nc.dram_tensor("xown", [TT, D], F32, kind="ExternalInput").ap()
        yall4 = ctx["yall"].rearrange("(k r t) c -> k t r c", r=4, t=CH)
    else:
        x_in = nc.dram_tensor("x", [TT, D], F32, kind="ExternalInput").ap()
        yT_in = nc.dram_tensor("yT", [D, TT], BF16, kind="ExternalInput").ap()
    if fused:
        w_out_in, w_up_in, w_dn_in, fg_in = ctx["w_out"], ctx["w_up"], ctx["w_dn"], ctx["ffn_g"]
    else:
        w_out_in = nc.dram_tensor("w_out", [D, D], F32, kind="ExternalInput").ap()
        w_up_in = nc.dram_tensor("w_up", [D, 2 * DFF], F32, kind="ExternalInput").ap()
        w_dn_in = nc.dram_tensor("w_dn", [DFF, D], F32, kind="ExternalInput").ap()
        fg_in = nc.dram_tensor("ffn_g", [128, 8], F32, kind="ExternalInput").ap()
    cw_in = nc.dram_tensor("conv_w", [128, 3, 2 * NFC], F32, kind="ExternalInput").ap()
    cb_in = nc.dram_tensor("conv_b", [128, 2 * NFC], F32, kind="ExternalInput").ap()
    og_in = nc.dram_tensor("final_g", [1, D], F32, kind="ExternalInput").ap()
    ident_in = ctx["ident_b"] if fused else nc.dram_tensor("ident", [128, 128], BF16, kind="ExternalInput").ap()
    out = nc.dram_tensor("out", [NTOK2, D], F32, kind="ExternalOutput").ap()
    wdn_scr = ctx["wdn_scr"] if fused else nc.dram_tensor("wdn_scr", [128, NFC, D], BF16, kind="Internal").ap()

    es = contextlib.ExitStack()
    with es:
        sb = lambda name, shape, dt: es.enter_context(nc.sbuf_tensor("f_sb_" + name, shape, dt))
        ps = lambda name, shape, dt: es.enter_context(nc.psum_tensor("f_ps_" + name, shape, dt))
        sc = Sched(nc, ctx["es_sem"] if fused else es, prefix="f_")

        def xrows(e, t0, n):
            return x_in[t0:t0 + n, :]

        _ysl = {}

        def yrows(e, t, n):
            kk, r0 = t // CH, t % CH
            key = (id(e), kk)
            if key not in _ysl:
                _ysl[key] = yall4[bass.ds(sc.core_c(e) * CPS + kk, 1), :, :, :]
            return _ysl[key][0, r0:r0 + n, :, :]

        ytm = [sb("ytm%d" % i, [128, 4, 256], BF16) for i in range(2)] if fused else None
        wout = sb("wout", [128, 8, D], BF16)
        wup = sb("wup", [128, 8, 2 * DFF], BF16)
        stage = [sb("stage%d" % i, [128, 1024], F32) for i in range(2)] if not fused else None
        wdst = [sb("wdst%d" % i, [128, 2, D], BF16) for i in range(3)]
        fg = sb("fg", [128, 8], F32)
        cw = sb("cw", [128, 3, 2 * NFC], F32)
        cb = sb("cb", [128, 2 * NFC], F32)
        ogb = sb("ogb", [128, D], F32)
        ident = sb("ident", [128, 128], BF16)
        xh = [sb("xh%d" % i, [128, 2, D], F32) for i in range(2)]
        yT = [sb("yTs%d" % i, [128, 8, GW], BF16) for i in range(2)]
        hn = [sb("hn%d" % i, [128, D], BF16) for i in range(2)]
        hnT2 = [sb("hnT%d" % i, [128, 8, GW], BF16) for i in range(2)]
        gT2 = [sb("gT%d" % i, [128, NFC, GW], BF16) for i in range(2)]
        uext = [sb("uext%d" % i, [128, 2 + GW], F32) for i in range(4)]
        cbuf = [sb("cbuf%d" % i, [128, GW], F32) for i in range(4)]
        sgb = [sb("sgb%d" % i, [128, GW], F32) for i in range(2)]
        carry = sb("carry", [128, 2 * NFC, 2], F32)
        ssq = sb("ssq", [128, 8], F32)
        rstd = sb("rstd", [128, 8], F32)
        junk = sb("junk", [128, D], BF16)

        p_t = ps("p_t", [128, 8, 128], BF16)
        p_u3 = [ps("p_u%d" % i, [128, 512], F32) for i in range(2)]
        p_h = ps("p_h", [128, 512], F32)
        p_d = [ps("p_d%d" % i, [128, 512], F32) for i in range(4)]
        p_d_all = [[p_d[0], p_d[1]], [p_d[2], p_d[3]]]
        epsb = sb("epsb", [128, 1], F32)
        sc.op("pool", lambda e: e.memset(epsb[:], EPS), writes=["epsb"])

        sc.dma("sp", "c0", lambda e: e.dma_start(out=fg[:], in_=fg_in), writes=["fg"])
        sc.dma("sp", "c1", lambda e: e.dma_start(out=cw[:], in_=cw_in), writes=["cw"])
        sc.dma("sp", "c2", lambda e: e.dma_start(out=cb[:], in_=cb_in), writes=["cb"])
        sc.dma("sp", "c3", lambda e: e.dma_start(out=ogb[:], in_=bcast_rows(og_in, 128)), writes=["ogb"])
        sc.dma("sp", "c4", lambda e: e.dma_start(out=ident[:], in_=ident_in), writes=["ident"])
        sc.op("pool", lambda e: e.memset(carry[:], 0.0), writes=["carry%d" % ch for ch in range(2 * NFC)])

        if fused:
            for k in range(8):
                sc.dma("sp", "wlo%d" % k, (lambda e, k=k: e.dma_start(out=wout[:, k, :], in_=ctx["wout_scr"][:, k, :])), writes=["wout%d" % k])
            for k in range(8):
                sc.dma("sp", "wlu%d" % k, (lambda e, k=k: e.dma_start(out=wup[:, k, :], in_=ctx["wup_scr"][:, k, :])), writes=["wup%d" % k])
        else:
            jobs = []
            for k in range(8):
                jobs.append((w_out_in[k * 128:(k + 1) * 128, :], wout[:, k, :], 1024, None, "wout%d" % k))
            for k in range(8):
                for c0 in range(0, 2 * DFF, 1024):
                    cwid = min(1024, 2 * DFF - c0)
                    jobs.append((w_up_in[k * 128:(k + 1) * 128, c0:c0 + cwid], wup[:, k, c0:c0 + cwid], cwid, k, "wup%d" % k))
            ji = 0
            for (src, dst, wid, gk, wtok) in jobs:
                s = ji % 2
                st = stage[s]
                sc.dma("sp", "stg%d" % s, (lambda e, st=st, src=src, wid=wid: e.dma_start(out=st[:, 0:wid], in_=src)), writes=["stage%d" % s])
                if ji % 2 == 0:
                    if gk is None:
                        sc.op("act", (lambda e, st=st, dst=dst, wid=wid: e.activation(out=dst, in_=st[:, 0:wid], func=AF.Copy)), reads=["stage%d" % s], writes=[wtok])
                    else:
                        sc.op("act", (lambda e, st=st, dst=dst, wid=wid, gk=gk: e.activation(out=dst, in_=st[:, 0:wid], func=AF.Copy, scale=fg[:, gk:gk + 1])),
                              reads=["stage%d" % s, "fg"], writes=[wtok])
                elif gk is None:
                    sc.op("dve", (lambda e, st=st, dst=dst, wid=wid: e.tensor_copy(out=dst, in_=st[:, 0:wid])), reads=["stage%d" % s], writes=[wtok])
                else:
                    sc.op("dve", (lambda e, st=st, dst=dst, wid=wid, gk=gk: e.tensor_scalar(out=dst, in0=st[:, 0:wid], scalar1=fg[:, gk:gk + 1], scalar2=None, op0=ALU.mult)),
                          reads=["stage%d" % s, "fg"], writes=[wtok])
                ji += 1
            for fc in range(NFC):
                s = ji % 2
                st = stage[s]
                sc.dma("sp", "stg%d" % s, (lambda e, st=st, fc=fc: e.dma_start(out=st[:], in_=w_dn_in[fc * 128:(fc + 1) * 128, :])), writes=["stage%d" % s])
                ws = (fc // 2) % 3
                a = fc % 2
                if ji % 2 == 0:
                    sc.op("act", (lambda e, st=st, ws=ws, a=a: e.activation(out=wdst[ws][:, a, :], in_=st[:], func=AF.Copy)), reads=["stage%d" % s], writes=["wdst%d" % ws])
                else:
                    sc.op("dve", (lambda e, st=st, ws=ws, a=a: e.tensor_copy(out=wdst[ws][:, a, :], in_=st[:])), reads=["stage%d" % s], writes=["wdst%d" % ws])
                if a == 1:
                    sc.dma("sp", "wsc%d" % ws, (lambda e, ws=ws, fc=fc: e.dma_start(out=wdn_scr[:, fc - 1:fc + 1, :], in_=wdst[ws][:])), reads=["wdst%d" % ws], writes=["wdn_scr"])
                ji += 1

        groups = [(0, HALO, False)] + [(HALO + g * GW, GW, True) for g in range(NG2)]

        def head(gi, part="all"):
            t0, W, real = groups[gi]
            nt = W // 128
            xb, yb, hT = xh[gi % 2], yT[gi % 2], hnT2[gi % 2]
            XH, YT, HT = "xh%d" % (gi % 2), "yT%d" % (gi % 2), "hnT%d" % (gi % 2)
            if part in ("all", "y"):
                head_y(gi, t0, W, nt, yb, YT)
            if part == "y":
                return
            sc.dma("sp", XH, (lambda e: e.dma_start(out=xb[:, 0:nt, :], in_=xrows(e, t0, nt * 128).rearrange("(a p) d -> p a d", p=128))), writes=[XH])
            head_x(gi, t0, W, nt, xb, yb, hT, XH, YT, HT)

        def head_y(gi, t0, W, nt, yb, YT):
            if not fused:
                sc.dma("sp", YT, (lambda e: e.dma_start(out=yb[:, :, 0:W], in_=yT_in[:, t0:t0 + W].rearrange("(k p) t -> p k t", p=128))), writes=[YT])
            else:
                for ti in range(nt):
                    ys = ytm[ti % 2]
                    YS = "ytm%d" % (ti % 2)
                    sc.dma("sp", YS, (lambda e, ys=ys, ti=ti: e.dma_start(out=ys[:], in_=yrows(e, t0 + ti * 128, 128))), writes=[YS])
                    for kk in range(8):
                        src = ys[:, kk, 0:128] if kk < 4 else ys[:, kk - 4, 128:256]
                        sc.op("pe", (lambda e, src=src, kk=kk: e.transpose(out=p_t[:, kk, :], in_=src, identity=ident[:])), reads=[YS, "ident"], writes=["p_t"], signal=(kk == 7))
                    sc.op("dve", (lambda e, ti=ti: e.tensor_copy(out=yb[:, :, ti * 128:(ti + 1) * 128], in_=p_t[:])), reads=["p_t"], writes=[YT])

        def head_x(gi, t0, W, nt, xb, yb, hT, XH, YT, HT):
            for ti in range(nt):
                for half in range(2):
                    ph = p_h
                    for k in range(8):
                        sc.op("pe", (lambda e, ph=ph, k=k, ti=ti, half=half: e.matmul(ph[:], lhsT=yb[:, k, ti * 128:(ti + 1) * 128], rhs=wout[:, k, half * 512:(half + 1) * 512], start=(k == 0), stop=(k == 7))),
                              reads=[YT, "wout%d" % k], writes=["p_h"], signal=(k == 7), hold=(k != 7))
                    sc.op("dve", (lambda e, ph=ph, ti=ti, half=half: e.tensor_tensor(out=xb[:, ti, half * 512:(half + 1) * 512], in0=ph[:], in1=xb[:, ti, half * 512:(half + 1) * 512], op=ALU.add)),
                          reads=["p_h"], writes=[XH])
                col = (gi % 2) * 2 + ti
                sc.op("act", (lambda e, ti=ti, col=col: e.activation(out=junk[:], in_=xb[:, ti, :], func=AF.Square, accum_out=ssq[:, col:col + 1])), reads=[XH], writes=["junk", "ssq%d" % col])
                sc.op("act", (lambda e, col=col: e.activation(out=rstd[:, col:col + 1], in_=ssq[:, col:col + 1], func=AF.Sqrt, scale=1.0 / D, bias=epsb[:, 0:1])), reads=["ssq%d" % col, "epsb"], writes=["rstd%d" % col])
                sc.op("dve", (lambda e, col=col: e.reciprocal(out=rstd[:, col:col + 1], in_=rstd[:, col:col + 1])), reads=["rstd%d" % col], writes=["rstd%d" % col])
                hb = hn[ti % 2]
                HN = "hn%d" % (ti % 2)
                sc.op("dve", (lambda e, hb=hb, ti=ti, col=col: e.tensor_scalar(out=hb[:], in0=xb[:, ti, :], scalar1=rstd[:, col:col + 1], scalar2=None, op0=ALU.mult)), reads=[XH, "rstd%d" % col], writes=[HN])
                for k in range(8):
                    sc.op("pe", (lambda e, hb=hb, k=k: e.transpose(out=p_t[:, k, :], in_=hb[:, k * 128:(k + 1) * 128], identity=ident[:])), reads=[HN, "ident"], writes=["p_t"], signal=(k == 7))
                sc.op("dve", (lambda e, ti=ti: e.tensor_copy(out=hT[:, :, ti * 128:(ti + 1) * 128], in_=p_t[:])), reads=["p_t"], writes=[HT])

        HW = 32

        def up(gi, bg):
            t0, W, real = groups[gi]
            hT, HT = hnT2[gi % 2], "hnT%d" % (gi % 2)
            pend = None
            for fc in range(NFC + 1):
                if fc < NFC:
                    cs = []
                    for hv in range(2):
                        ch = hv * NFC + fc
                        pi = (2 * fc + hv) % 4
                        pq = (2 * fc + hv) % 2
                        pu = p_u3[pq]
                        for k in range(8):
                            if real:
                                sc.op("pe", (lambda e, pu=pu, k=k, ch=ch: e.matmul(pu[:, 0:W], lhsT=wup[:, k, ch * 128:(ch + 1) * 128], rhs=hT[:, k, 0:W], start=(k == 0), stop=(k == 7))),
                                      reads=[HT, "wup%d" % k], writes=["p_u%d" % pq], signal=(k == 7))
                            else:
                                sc.op("pe", (lambda e, pu=pu, k=k, ch=ch: e.matmul(pu[:, 0:HW], lhsT=wup[:, k, ch * 128:(ch + 1) * 128], rhs=hT[:, k, W - HW:W], start=(k == 0), stop=(k == 7))),
                                      reads=[HT, "wup%d" % k], writes=["p_u%d" % pq], signal=(k == 7))
                        ub, UB = uext[pi], "uext%d" % pi
                        cbf, CB = cbuf[pi], "cbuf%d" % pi
                        if not real:
                            sc.op("act", (lambda e, ub=ub, pu=pu: e.activation(out=ub[:, 2:2 + HW], in_=pu[:, 0:HW], func=AF.Copy)), reads=["p_u%d" % pq], writes=[UB])
                            sc.op("pool", (lambda e, ub=ub, ch=ch: e.tensor_copy(out=carry[:, ch, :], in_=ub[:, HW:HW + 2])), reads=[UB], writes=["carry%d" % ch])
                            cs.append((cbf, CB))
                            continue
                        sc.op("pool", (lambda e, ub=ub, ch=ch: e.tensor_copy(out=ub[:, 0:2], in_=carry[:, ch, :])), reads=["carry%d" % ch], writes=[UB])
                        sc.op("act", (lambda e, ub=ub, pu=pu: e.activation(out=ub[:, 2:2 + W], in_=pu[:, 0:W], func=AF.Copy)), reads=["p_u%d" % pq], writes=[UB])
                        sc.op("act", (lambda e, cbf=cbf, pu=pu, ch=ch: e.activation(out=cbf[:, 0:W], in_=pu[:, 0:W], func=AF.Identity, scale=cw[:, 2, ch:ch + 1], bias=cb[:, ch:ch + 1])), reads=["p_u%d" % pq, "cw", "cb"], writes=[CB])
                        sc.op("pool", (lambda e, ub=ub, ch=ch: e.tensor_copy(out=carry[:, ch, :], in_=ub[:, W:W + 2])), reads=[UB], writes=["carry%d" % ch])
                        sc.op("dve", (lambda e, cbf=cbf, ub=ub, ch=ch: e.scalar_tensor_tensor(out=cbf[:, 0:W], in0=ub[:, 1:1 + W], scalar=cw[:, 1, ch:ch + 1], in1=cbf[:, 0:W], op0=ALU.mult, op1=ALU.add)), reads=[UB, CB, "cw"], writes=[CB])
                        sc.op("dve", (lambda e, cbf=cbf, ub=ub, ch=ch: e.scalar_tensor_tensor(out=cbf[:, 0:W], in0=ub[:, 0:W], scalar=cw[:, 0, ch:ch + 1], in1=cbf[:, 0:W], op0=ALU.mult, op1=ALU.add)), reads=[UB, CB, "cw"], writes=[CB])
                        cs.append((cbf, CB))
                if pend is not None and real:
                    pfc, pcs = pend
                    sg = sgb[pfc % 2]
                    SG = "sgb%d" % (pfc % 2)
                    sc.op("act", (lambda e, sg=sg, c0=pcs[0][0]: e.activation(out=sg[:, 0:W], in_=c0[:, 0:W], func=AF.Silu)), reads=[pcs[0][1]], writes=[SG])
                    sc.op("pool", (lambda e, sg=sg, c1=pcs[1][0], pfc=pfc: e.tensor_tensor(out=gT2[gi % 2][:, pfc, 0:W], in0=sg[:, 0:W], in1=c1[:, 0:W], op=ALU.mult)), reads=[SG, pcs[1][1]], writes=["gT%d" % (gi % 2)])
                pend = (fc, cs) if fc < NFC else None
                if bg:
                    sc.run_deferred(bg, 0, iters_left=NFC + 1 - fc)

        def tail(gi):
            t0, W, real = groups[gi]
            nt = W // 128
            xb, XH = xh[gi % 2], "xh%d" % (gi % 2)
            for fc in range(0, NFC, 2):
                ws = (fc // 2) % 3
                sc.dma("sp", "wdl%d" % ws, (lambda e, ws=ws, fc=fc: e.dma_start(out=wdst[ws][:], in_=wdn_scr[:, fc:fc + 2, :])), reads=["wdn_scr"], writes=["wdst%d" % ws])
                for ti in range(nt):
                    for half in range(2):
                        pd = p_d_all[ti][half]
                        for a in range(2):
                            sc.op("pe", (lambda e, pd=pd, ws=ws, a=a, fc=fc, ti=ti, half=half: e.matmul(pd[:], lhsT=gT2[gi % 2][:, fc + a, ti * 128:(ti + 1) * 128], rhs=wdst[ws][:, a, half * 512:(half + 1) * 512], start=(fc + a == 0), stop=(fc + a == NFC - 1))),
                                  reads=["gT%d" % (gi % 2), "wdst%d" % ws], writes=["p_d%d_%d" % (ti, half)],
                                  signal=(a == 1 and half == 1 and ti == nt - 1) or (fc + a == NFC - 1))
            for ti in range(nt):
                for half in range(2):
                    pd = p_d_all[ti][half]
                    sc.op("dve", (lambda e, pd=pd, ti=ti, half=half: e.tensor_tensor(out=xb[:, ti, half * 512:(half + 1) * 512], in0=pd[:], in1=xb[:, ti, half * 512:(half + 1) * 512], op=ALU.add)),
                          reads=["p_d%d_%d" % (ti, half)], writes=[XH])
                col = 4 + ti
                sc.op("act", (lambda e, ti=ti, col=col: e.activation(out=junk[:], in_=xb[:, ti, :], func=AF.Square, accum_out=ssq[:, col:col + 1])), reads=[XH], writes=["junk", "ssq%d" % col])
                sc.op("act", (lambda e, col=col: e.activation(out=rstd[:, col:col + 1], in_=ssq[:, col:col + 1], func=AF.Sqrt, scale=1.0 / D, bias=epsb[:, 0:1])), reads=["ssq%d" % col, "epsb"], writes=["rstd%d" % col])
                sc.op("dve", (lambda e, col=col: e.reciprocal(out=rstd[:, col:col + 1], in_=rstd[:, col:col + 1])), reads=["rstd%d" % col], writes=["rstd%d" % col])
                sc.op("dve", (lambda e, ti=ti, col=col: e.scalar_tensor_tensor(out=xb[:, ti, :], in0=xb[:, ti, :], scalar=rstd[:, col:col + 1], in1=ogb[:], op0=ALU.mult, op1=ALU.mult)), reads=["rstd%d" % col, "ogb"], writes=[XH])
            r0 = t0 - HALO
            sc.dma("sp", "ost%d" % (gi % 2), (lambda e, r0=r0: e.dma_start(out=out[r0:r0 + nt * 128, :].rearrange("(a p) d -> p a d", p=128), in_=xb[:, 0:nt, :])), reads=[XH], writes=["out"])

        head(0)
        for gi in range(len(groups)):
            def rec2(fn, *a):
                lst = []
                sc.defer = lst
                fn(*a)
                sc.defer = None
                return lst
            bg = ((rec2(head, gi + 1, "y") if gi + 1 < len(groups) else [])
                  + (rec2(tail, gi - 1) if (gi >= 1 and groups[gi - 1][2]) else [])
                  + (rec2(head, gi + 1, "x") if gi + 1 < len(groups) else []))
            up(gi, bg)
            sc.run_deferred(bg, len(bg))
        tail(len(groups) - 1)
        sc.wait_all("sp", ["out"])
        with nc.Block() as block:
            sc.emit(block, pre_waits=(ctx["pre_waits"] if fused else ()))
    return nc


def _bf16(a):
    return a.astype(ml_dtypes.bfloat16)


def ffn_inputs(x, yT_full, w_out, w_ffn_up, w_ffn_down, ffn_norm_g, conv_w, conv_b, final_norm_g):
    ident = np.eye(128, dtype=np.float32).astype(ml_dtypes.bfloat16)
    shared = {
        "w_out": np.ascontiguousarray(w_out[0]),
        "w_up": np.ascontiguousarray(w_ffn_up[0]),
        "w_dn": np.ascontiguousarray(w_ffn_down[0]),
        "ffn_g": np.ascontiguousarray(ffn_norm_g[0].reshape(8, 128).T),
        "conv_w": np.ascontiguousarray(conv_w[0].reshape(3, 2 * NFC, 128).transpose(2, 0, 1)),
        "conv_b": np.ascontiguousarray(conv_b[0].reshape(2 * NFC, 128).T),
        "final_g": np.ascontiguousarray(final_norm_g.reshape(1, D)),
        "ident": ident,
    }
    maps = []
    for core in range(8):
        b, c = core // 4, core % 4
        t0 = c * NTOK2
        xs = np.zeros((HALO + NTOK2, D), np.float32)
        ys = np.zeros((D, HALO + NTOK2), ml_dtypes.bfloat16)
        xs[HALO:] = x[b, t0:t0 + NTOK2]
        ys[:, HALO:] = yT_full[b][:, t0:t0 + NTOK2]
        if c > 0:
            xs[:HALO] = x[b, t0 - HALO:t0]
            ys[:, :HALO] = yT_full[b][:, t0 - HALO:t0]
        m = dict(shared)
        m["x"] = xs
        m["yT"] = ys
        maps.append(m)
    return maps


S1 = int(os.environ.get('K_S1', S))
NT1 = S1 // 128
NG1 = S1 // 512
NCA = 400
NCB = 384
BIG = 1.0e30


def build_mixer(ctx=None):
    fused = ctx is not None
    nc = ctx["nc"] if fused else bass.Bass("TRN2", target_bir_lowering=False)
    if fused:
        x_in = ctx["xpad"][HALO:HALO + S1, :]
    else:
        x_in = nc.dram_tensor("x", [S1, D], F32, kind="ExternalInput").ap()
    ag_in = nc.dram_tensor("attn_g", [128, 8], F32, kind="ExternalInput").ap()
    w_in = nc.dram_tensor("w", [D, NCA + NCB], F32, kind="ExternalInput").ap()
    wg_in = nc.dram_tensor("wg", [16, 64], F32, kind="ExternalInput").ap()
    bg_in = nc.dram_tensor("bg", [1, 64], F32, kind="ExternalInput").ap()
    gng_in = nc.dram_tensor("gng", [1, 128], F32, kind="ExternalInput").ap()
    cos_in = nc.dram_tensor("cos4", [128, NT1, 128], F32, kind="ExternalInput").ap()
    sin_in = nc.dram_tensor("sin4", [128, NT1, 128], F32, kind="ExternalInput").ap()
    ehot_in = nc.dram_tensor("ehot", [32, S1], BF16, kind="ExternalInput").ap()
    addtab_in = nc.dram_tensor("addtab", [1, 32 * 32], F32, kind="ExternalInput").ap()
    trif_in = nc.dram_tensor("tri_f", [128, 128], F32, kind="ExternalInput").ap()
    trib_in = nc.dram_tensor("tri_b", [128, 128], BF16, kind="ExternalInput").ap()
    identb_in = nc.dram_tensor("ident_b", [128, 128], BF16, kind="ExternalInput").ap()
    identf_in = nc.dram_tensor("ident_f", [128, 128], F32, kind="ExternalInput").ap()
    if fused:
        ctx["ident_b"] = identb_in
        y_out = ctx["ybuf"][HALO:HALO + S1, :]
    else:
        y_out = nc.dram_tensor("y", [S1, 256], BF16, kind="ExternalOutput").ap()

    es = contextlib.ExitStack()
    with es:
        sb = lambda name, shape, dt: es.enter_context(nc.sbuf_tensor("m_sb_" + name, shape, dt))
        ps = lambda name, shape, dt: es.enter_context(nc.psum_tensor("m_ps_" + name, shape, dt))
        sc = Sched(nc, ctx["es_sem"] if fused else es, prefix="m_")

        Kx = [sb("Kx%d" % h, [96, S1], BF16) for h in range(2)]
        Vx = sb("Vx", [128, NT1, 2, 72], BF16)
        kms = [sb("kms%d" % h, [64, 32], BF16) for h in range(2)]
        kmsf = [sb("kmsf%d" % h, [64, 32], F32) for h in range(2)]
        wbf = sb("wbf", [128, 8, NCA + NCB], BF16)
        wstage = [sb("wstage%d" % i, [128, NCA + NCB], F32) for i in range(2)]
        ag = sb("ag", [128, 8], F32)
        wgf = sb("wgf", [16, 64], F32)
        wgb = sb("wgb", [16, 64], BF16)
        bgb = sb("bgb", [128, 64], F32)
        gngb = sb("gngb", [128, 128], F32)
        addtab = sb("addtab", [128, 32, 32], F32)
        trif = sb("trif", [128, 128], F32)
        trib = sb("trib", [128, 128], BF16)
        identb = sb("identb", [128, 128], BF16)
        identf = sb("identf", [128, 128], F32)
        epsb = sb("epsb", [128, 1], F32)
        Sst = sb("Sst", [64, 128], F32)
        Sbf = sb("Sbf", [64, 128], BF16)
        xt = [sb("xt%d" % i, [128, D], F32) for i in range(2)]
        xnb = [sb("xnb%d" % i, [128, D], BF16) for i in range(2)]
        xnT = [sb("xnT%d" % i, [128, 8, 128], BF16) for i in range(2)]
        junk = sb("junk", [128, D], BF16)
        st1 = sb("st1", [128, 16], F32)
        cos4 = [sb("cos4_%d" % i, [128, 4, 128], F32) for i in range(2)]
        sin4 = [sb("sin4_%d" % i, [128, 4, 128], F32) for i in range(2)]
        rtmp = [sb("rtmp%d" % i, [128, 128], F32) for i in range(4)]
        qk_tm = [sb("qk_tm%d" % i, [128, 256], BF16) for i in range(2)]
        g_tm = [sb("g_tm%d" % i, [128, 144], BF16) for i in range(2)]
        gv_tm = [sb("gv_tm%d" % i, [128, 128], BF16) for i in range(2)]
        sgr = [sb("sgr%d" % i, [128, 128], F32) for i in range(2)]
        sgt = [sb("sgt%d" % i, [128, 128], F32) for i in range(2)]
        Qx = [[sb("Qx%d_%d" % (h, i), [96, 512], BF16) for i in range(2)] for h in range(2)]
        glrT2 = [sb("glrT%d" % i, [16, 128], BF16) for i in range(2)]
        gqk = [sb("gqk%d" % i, [64, 2, 128], BF16) for i in range(2)]
        zb = sb("zb", [128, 64], F32)
        lsp = sb("lsp", [128, 64], F32)
        eG = sb("eG", [64, 128], F32)
        eGi = sb("eGi", [64, 128], F32)
        qtT = sb("qtT", [64, 128], BF16)
        ktT = sb("ktT", [64, 128], BF16)
        kt_tm = sb("kt_tm", [128, 64], BF16)
        AT = sb("AT", [128, 128], BF16)
        stmp = sb("stmp", [64, 128], F32)
        ogt = sb("ogt", [128, 128], F32)
        gm2 = sb("gm2", [128, 2, 32], F32)
        m82 = sb("m82", [128, 2, 8], F32)
        thr2 = sb("thr2", [128, 2], F32)
        Mpad2 = [sb("Mpad%d" % h, [128, 96], BF16) for h in range(2)]
        PT = [sb("PT%d" % i, [128, 512], BF16) for i in range(3)]
        OT = sb("OT", [65, 512], F32)
        rcp = sb("rcp", [128, 1], F32)
        rcp2 = sb("rcp2", [128, 2], F32)
        y_tm = [sb("y_tm%d" % i, [128, 4, 256], BF16) for i in range(2)]

        pa = ps("pa", [128, 512], F32)
        pb = ps("pb", [128, 512], F32)
        pT = ps("pT", [128, 8, 128], BF16)
        pxT = ps("pxT", [128, 8, 128], BF16)
        pS = [ps("pS%d" % i, [128, 512], F32) for i in range(2)]
        pO = ps("pO", [128, 512], F32)
        pM = ps("pM", [128, 512], F32)
        pMb = pM

        cl = [("ag", ag, ag_in), ("wgf", wgf, wg_in), ("trif", trif, trif_in), ("trib", trib, trib_in),
              ("identb", identb, identb_in), ("identf", identf, identf_in)]
        for i, (tk, t, src) in enumerate(cl):
            sc.dma("sp", "k%d" % i, (lambda e, t=t, src=src: e.dma_start(out=t[:], in_=src)), writes=[tk])
        sc.dma("sp", "kb0", lambda e: e.dma_start(out=bgb[:], in_=bcast_rows(bg_in, 128)), writes=["bgb"])
        sc.dma("sp", "kb1", lambda e: e.dma_start(out=gngb[:], in_=bcast_rows(gng_in, 128)), writes=["gngb"])
        sc.dma("sp", "kb2", lambda e: e.dma_start(out=addtab[:].rearrange("p a b -> p (a b)"), in_=bcast_rows(addtab_in, 128)), writes=["addtab"])
        for h in range(2):
            sc.dma("sp", "ke%d" % h, (lambda e, h=h: e.dma_start(out=Kx[h][64:96, :], in_=ehot_in)), writes=["KxE%d" % h])
        sc.op("pool", lambda e: e.memset(epsb[:], EPS), writes=["epsb"])
        sc.op("pool", lambda e: e.memset(Vx[:].rearrange("p a b c -> p (a b c)"), 1.0), writes=["Vx"])
        for h in range(2):
            sc.op("pool", (lambda e, h=h: e.memset(kms[h][:], 0.0)), writes=["kms%d" % h])
            for i in range(2):
                sc.op("pool", (lambda e, h=h, i=i: e.memset(Qx[h][i][:], 0.0)), writes=["Qx%d_%d" % (h, i)])
        for h in range(2):
            sc.op("pool", (lambda e, h=h: e.memset(Mpad2[h][:], 0.0)), writes=["Mpad%d" % h])
        sc.op("pool", lambda e: e.memset(Sst[:], 0.0), writes=["Sst"])
        sc.op("pool", lambda e: e.memset(Sbf[:], 0.0), writes=["Sbf"])
        sc.op("dve", lambda e: e.tensor_copy(out=wgb[:], in_=wgf[:]), reads=["wgf"], writes=["wgb"])
        for k in range(8):
            s = k % 2
            sc.dma("sp", "ws%d" % s, (lambda e, s=s, k=k: e.dma_start(out=wstage[s][:], in_=w_in[k * 128:(k + 1) * 128, :])), writes=["wstage%d" % s])
            if k % 2:
                sc.op("act", (lambda e, s=s, k=k: e.activation(out=wbf[:, k, :], in_=wstage[s][:], func=AF.Copy, scale=ag[:, k:k + 1])), reads=["wstage%d" % s, "ag"], writes=["wbf%d" % k])
            else:
                sc.op("dve", (lambda e, s=s, k=k: e.tensor_scalar(out=wbf[:, k, :], in0=wstage[s][:], scalar1=ag[:, k:k + 1], scalar2=None, op0=ALU.mult)),
                      reads=["wstage%d" % s, "ag"], writes=["wbf%d" % k])

        STOP = int(os.environ.get("K_STOP", 99))

        def proj_tile(g, ti):
            T = g * 4 + ti
            s2 = T % 2
            XT, XNB, XNT = "xt%d" % s2, "xnb%d" % s2, "xnT%d" % s2
            c0, c1 = (T % 4) * 2, (T % 4) * 2 + 1
            sc.dma("sp", XT, (lambda e: e.dma_start(out=xt[s2][:], in_=x_in[T * 128:(T + 1) * 128, :])), writes=[XT])
            sc.op("act", (lambda e: e.activation(out=junk[:], in_=xt[s2][:], func=AF.Square, accum_out=st1[:, c0:c0 + 1])), reads=[XT], writes=["junk", "st1_%d" % c0])
            sc.op("act", (lambda e: e.activation(out=st1[:, c1:c1 + 1], in_=st1[:, c0:c0 + 1], func=AF.Ln, scale=1.0 / D, bias=epsb[:, 0:1])), reads=["st1_%d" % c0, "epsb"], writes=["st1_%d" % c1])
            sc.op("act", (lambda e: e.activation(out=st1[:, c1:c1 + 1], in_=st1[:, c1:c1 + 1], func=AF.Exp, scale=-0.5)), reads=["st1_%d" % c1], writes=["st1_%d" % c1])
            sc.op("dve", (lambda e: e.tensor_scalar(out=xnb[s2][:], in0=xt[s2][:], scalar1=st1[:, c1:c1 + 1], scalar2=None, op0=ALU.mult)), reads=[XT, "st1_%d" % c1], writes=[XNB])
            for k in range(8):
                sc.op("pe", (lambda e, k=k: e.transpose(out=pxT[:, k, :], in_=xnb[s2][:, k * 128:(k + 1) * 128], identity=identb[:])), reads=[XNB, "identb"], writes=["pxT"], signal=(k == 7))
            sc.op("dve", (lambda e: e.tensor_copy(out=xnT[s2][:], in_=pxT[:])), reads=["pxT"], writes=[XNT])
            if STOP < 2:
                return
            for k in range(8):
                sc.op("pe", (lambda e, k=k: e.matmul(pa[:, 0:NCA], lhsT=xnT[s2][:, k, :], rhs=wbf[:, k, 0:NCA], start=(k == 0), stop=(k == 7))), reads=[XNT, "wbf%d" % k], writes=["pa"], signal=(k == 7), hold=(k != 7))
            for k in range(8):
                sc.op("pe", (lambda e, k=k: e.matmul(pb[:, 0:NCB], lhsT=xnT[s2][:, k, :], rhs=wbf[:, k, NCA:NCA + NCB], start=(k == 0), stop=(k == 7))), reads=[XNT, "wbf%d" % k], writes=["pb"], signal=(k == 7), hold=(k != 7))
            if STOP < 3:
                return
            gs = g % 2
            if ti == 0:
                sc.dma("sp", "cos%d" % gs, (lambda e: e.dma_start(out=cos4[gs][:], in_=cos_in[:, g * 4:(g + 1) * 4, :])), writes=["cos%d" % gs])
                sc.dma("sp", "sin%d" % gs, (lambda e: e.dma_start(out=sin4[gs][:], in_=sin_in[:, g * 4:(g + 1) * 4, :])), writes=["sin%d" % gs])
            pav = pa[:, 0:256].rearrange("p (a b c) -> p a b c", a=4, b=2)
            cv = cos4[gs][:, ti, :].rearrange("p (a c) -> p a c", a=4)
            sv = sin4[gs][:, ti, :].rearrange("p (a c) -> p a c", a=4)
            qkv = qk_tm[s2][:].rearrange("p (a b c) -> p a b c", a=4, b=2)
            rv = [rtmp[i][:].rearrange("p (a c) -> p a c", a=4) for i in range(4)]
            CS = ["cos%d" % gs, "sin%d" % gs]
            QK = "qk_tm%d" % s2
            sc.op("dve", (lambda e: e.tensor_tensor(out=rv[0], in0=pav[:, :, 0, :], in1=cv, op=ALU.mult)), reads=["pa"] + CS, writes=["rtmp0"])
            sc.op("dve", (lambda e: e.tensor_tensor(out=rv[1], in0=pav[:, :, 1, :], in1=sv, op=ALU.mult)), reads=["pa"] + CS, writes=["rtmp1"])
            sc.op("dve", (lambda e: e.tensor_tensor(out=qkv[:, :, 0, :], in0=rv[0], in1=rv[1], op=ALU.subtract)), reads=["rtmp0", "rtmp1"], writes=[QK + "a"])
            sc.op("dve", (lambda e: e.tensor_tensor(out=rv[2], in0=pav[:, :, 1, :], in1=cv, op=ALU.mult)), reads=["pa"] + CS, writes=["rtmp2"])
            sc.op("dve", (lambda e: e.tensor_tensor(out=rv[3], in0=pav[:, :, 0, :], in1=sv, op=ALU.mult)), reads=["pa"] + CS, writes=["rtmp3"])
            sc.op("dve", (lambda e: e.tensor_tensor(out=qkv[:, :, 1, :], in0=rv[2], in1=rv[3], op=ALU.add)), reads=["rtmp2", "rtmp3"], writes=[QK + "b"])
            if STOP < 4:
                return
            GT = "g_tm%d" % s2
            sc.op("dve", (lambda e: e.tensor_copy(out=g_tm[s2][:], in_=pa[:, 256:400])), reads=["pa"], writes=[GT])
            sc.op("dve", (lambda e: e.tensor_copy(out=Vx[:, T, :, 0:64], in_=pb[:, 0:128].rearrange("p (h d) -> p h d", h=2))), reads=["pb", "Vx"], writes=["Vx_%d" % g])
            GV = "gv_tm%d" % s2
            sc.op("dve", (lambda e: e.tensor_copy(out=gv_tm[s2][:], in_=pb[:, 128:256])), reads=["pb"], writes=[GV])
            SGT, SGR = "sgt%d" % s2, "sgr%d" % s2
            sc.op("act", (lambda e: e.activation(out=sgt[s2][:], in_=pb[:, 256:384], func=AF.Exp, scale=-1.0)), reads=["pb"], writes=[SGT])
            sc.op("dve", (lambda e: e.tensor_copy(out=sgr[s2][:], in_=pb[:, 256:384])), reads=["pb"], writes=[SGR])
            if STOP < 5:
                return
            for i in range(4):
                sc.op("pe", (lambda e, i=i: e.transpose(out=pT[0:64, i, :], in_=qk_tm[s2][:, i * 64:(i + 1) * 64], identity=identb[:])), reads=[QK + "a", QK + "b", "identb"], writes=["pT"], signal=False)
            for i in range(2):
                sc.op("pe", (lambda e, i=i: e.transpose(out=pT[0:64, 4 + i, :], in_=g_tm[s2][:, i * 64:(i + 1) * 64], identity=identb[:])), reads=[GT, "identb"], writes=["pT"], signal=False)
            sc.op("pe", (lambda e: e.transpose(out=pT[0:16, 6, :], in_=g_tm[s2][:, 128:144], identity=identb[:])), reads=[GT, "identb"], writes=["pT"], signal=True)
            qi = g % 2
            sc.op("dve", (lambda e: e.tensor_copy(out=glrT2[s2][:], in_=pT[0:16, 6, :])), reads=["pT"], writes=["glrT%d" % s2])
            sc.op("dve", (lambda e: e.tensor_copy(out=gqk[s2][:], in_=pT[0:64, 4:6, :])), reads=["pT"], writes=["gqk%d" % s2])
            for h in range(2):
                sc.op("dve", (lambda e, h=h: e.tensor_copy(out=Qx[h][qi][0:64, ti * 128:(ti + 1) * 128], in_=pT[0:64, h, :])),
                      reads=["pT"], writes=["Qx%d_%d" % (h, qi)])
                sc.op("dve", (lambda e, h=h: e.tensor_copy(out=Kx[h][0:64, T * 128:(T + 1) * 128], in_=pT[0:64, 2 + h, :])),
                      reads=["pT"], writes=["Kx%d_%d" % (h, g)])
            if STOP < 6:
                return
            if T % 2 == 1:
                n = T // 2
                for h in range(2):
                    sc.op("dve", (lambda e, h=h: e.tensor_reduce(out=kmsf[h][:, n:n + 1], in_=Kx[h][0:64, n * 256:(n + 1) * 256], axis=AX.X, op=ALU.add)), reads=["Kx%d_%d" % (h, g)], writes=["kmsf%d" % h])
                    sc.op("dve", (lambda e, h=h: e.tensor_copy(out=kms[h][:, n:n + 1], in_=kmsf[h][:, n:n + 1])), reads=["kmsf%d" % h], writes=["kms%d" % h])

        def gla_chunk(g, ti):
            T = g * 4 + ti
            s2 = T % 2
            GV, SGR = "gv_tm%d" % s2, "sgr%d" % s2
            sc.op("pe", (lambda e: e.matmul(pM[:, 0:64], lhsT=glrT2[s2][:], rhs=wgb[:], start=True, stop=True)), reads=["glrT%d" % s2, "wgb"], writes=["pM"])
            sc.op("dve", (lambda e: e.tensor_tensor(out=zb[:], in0=pM[:, 0:64], in1=bgb[:], op=ALU.add)), reads=["pM", "bgb"], writes=["zb"])
            sc.op("act", (lambda e: e.activation(out=zb[:], in_=zb[:], func=AF.Exp, scale=-1.0)), reads=["zb"], writes=["zb"])
            sc.op("act", (lambda e: e.activation(out=lsp[:], in_=zb[:], func=AF.Ln, bias=oneb[:, 0:1])), reads=["zb", "oneb"], writes=["lsp"])
            sc.op("pe", (lambda e: e.matmul(pM[0:64, 128:256], lhsT=lsp[:], rhs=trif[:], start=True, stop=True)), reads=["lsp", "trif"], writes=["pM"])
            sc.op("act", (lambda e: e.activation(out=eG[:], in_=pM[0:64, 128:256], func=AF.Exp, scale=-1.0 / 16.0)), reads=["pM"], writes=["eG"])
            sc.op("act", (lambda e: e.activation(out=eGi[:], in_=pM[0:64, 128:256], func=AF.Exp, scale=1.0 / 16.0)), reads=["pM"], writes=["eGi"])
            sc.op("dve", (lambda e: e.scalar_tensor_tensor(out=qtT[:], in0=gqk[s2][:, 0, :], scalar=0.125, in1=eG[:], op0=ALU.mult, op1=ALU.mult)), reads=["gqk%d" % s2, "eG"], writes=["qtT"])
            sc.op("dve", (lambda e: e.tensor_tensor(out=ktT[:], in0=gqk[s2][:, 1, :], in1=eGi[:], op=ALU.mult)), reads=["gqk%d" % s2, "eGi"], writes=["ktT"])
            sc.op("pe", (lambda e: e.transpose(out=pT[:, 7, 0:64], in_=ktT[:], identity=identb[0:64, 0:64])), reads=["ktT", "identb"], writes=["pT"])
            sc.op("dve", (lambda e: e.tensor_copy(out=kt_tm[:], in_=pT[:, 7, 0:64])), reads=["pT"], writes=["kt_tm"])
            sc.op("pe", (lambda e: e.matmul(pM[:, 256:384], lhsT=ktT[:], rhs=qtT[:], start=True, stop=True)), reads=["ktT", "qtT"], writes=["pM"])
            sc.op("dve", (lambda e: e.tensor_tensor(out=AT[:], in0=pM[:, 256:384], in1=trif[:], op=ALU.mult)), reads=["pM", "trif"], writes=["AT"])
            sc.op("pe", (lambda e: e.matmul(pM[:, 384:512], lhsT=AT[:], rhs=gv_tm[s2][:], start=True, stop=False)), reads=["AT", GV], writes=["pM"], signal=False, hold=True)
            sc.op("pe", (lambda e: e.matmul(pM[:, 384:512], lhsT=qtT[:], rhs=Sbf[:], start=False, stop=True)), reads=["qtT", "Sbf"], writes=["pM"])
            c0, c1 = 8 + (T % 2) * 2, 9 + (T % 2) * 2
            sc.op("act", (lambda e: e.activation(out=junk[:, 0:128], in_=pM[:, 384:512], func=AF.Square, accum_out=st1[:, c0:c0 + 1])), reads=["pM"], writes=["junk", "st1_%d" % c0])
            sc.op("act", (lambda e: e.activation(out=st1[:, c1:c1 + 1], in_=st1[:, c0:c0 + 1], func=AF.Ln, scale=1.0 / 128.0, bias=epsb[:, 0:1])), reads=["st1_%d" % c0, "epsb"], writes=["st1_%d" % c1])
            sc.op("act", (lambda e: e.activation(out=st1[:, c1:c1 + 1], in_=st1[:, c1:c1 + 1], func=AF.Exp, scale=-0.5)), reads=["st1_%d" % c1], writes=["st1_%d" % c1])
            sc.op("dve", (lambda e: e.scalar_tensor_tensor(out=ogt[:], in0=pM[:, 384:512], scalar=st1[:, c1:c1 + 1], in1=gngb[:], op0=ALU.mult, op1=ALU.mult)), reads=["pM", "st1_%d" % c1, "gngb"], writes=["ogt"])
            YT = "y_tm%d" % (g % 2)
            SGT = "sgt%d" % s2
            sc.op("dve", (lambda e: e.tensor_scalar(out=sgt[s2][:], in0=sgt[s2][:], scalar1=1.0, scalar2=None, op0=ALU.add)), reads=[SGT], writes=[SGT])
            sc.op("dve", (lambda e: e.reciprocal(out=sgt[s2][:], in_=sgt[s2][:])), reads=[SGT], writes=[SGT])
            sc.op("pool", (lambda e: e.tensor_tensor(out=sgr[s2][:], in0=sgr[s2][:], in1=sgt[s2][:], op=ALU.mult)), reads=[SGR, SGT], writes=[SGR])
            sc.op("pool", (lambda e: e.tensor_tensor(out=y_tm[g % 2][:, ti, 128:256], in0=ogt[:], in1=sgr[s2][:], op=ALU.mult)), reads=["ogt", SGR], writes=[YT + "g%d" % ti])
            sc.op("pe", (lambda e: e.matmul(pM[0:64, 0:128], lhsT=kt_tm[:], rhs=gv_tm[s2][:], start=True, stop=True)), reads=["kt_tm", GV], writes=["pM"])
            sc.op("dve", (lambda e: e.tensor_scalar(out=stmp[:], in0=pM[0:64, 0:128], scalar1=eG[:, 127:128], scalar2=None, op0=ALU.mult)), reads=["pM", "eG"], writes=["stmp"])
            sc.op("dve", (lambda e: e.scalar_tensor_tensor(out=Sst[:], in0=Sst[:], scalar=eG[:, 127:128], in1=stmp[:], op0=ALU.mult, op1=ALU.add)), reads=["Sst", "eG", "stmp"], writes=["Sst"])
            sc.op("pool", (lambda e: e.tensor_copy(out=Sbf[:], in_=Sst[:])), reads=["Sst"], writes=["Sbf"])

        def gate_tile(g, ti):
            T = g * 4 + ti
            own = T // 2
            qi = g % 2
            QXS = ["Qx%d_%d" % (h, qi) for h in range(2)]
            for h in range(2):
                sc.op("pe", (lambda e, h=h: e.matmul(pM[:, h * 32:(h + 1) * 32], lhsT=Qx[h][qi][0:64, ti * 128:(ti + 1) * 128], rhs=kms[h][:], start=True, stop=True)),
                      reads=[QXS[h], "kms%d" % h], writes=["pM"], signal=(h == 1))
            for h in range(2):
                sc.op("dve", (lambda e, h=h: e.tensor_tensor(out=gm2[:, h, :], in0=pM[:, h * 32:(h + 1) * 32], in1=addtab[:, own, :], op=ALU.add)), reads=["pM", "addtab"], writes=["gm%d" % h])
            for h in range(2):
                sc.op("dve", (lambda e, h=h: e.max(out=m82[:, h, :], in_=gm2[:, h, :])), reads=["gm%d" % h], writes=["m8_%d" % h])
            sc.op("dve", (lambda e: e.tensor_scalar(out=thr2[:], in0=m82[:, :, 3], scalar1=-1.0e29, scalar2=None, op0=ALU.max)), reads=["m8_0", "m8_1"], writes=["thr"])
            for h in range(2):
                sc.op("dve", (lambda e, h=h: e.tensor_scalar(out=Mpad2[h][:, 64:96], in0=gm2[:, h, :], scalar1=thr2[:, h:h + 1], scalar2=NEG, op0=ALU.is_lt, op1=ALU.mult)), reads=["gm%d" % h, "thr"], writes=["Mpad%d" % h])
            for h in range(2):
                sc.op("pe", (lambda e, h=h: e.transpose(out=pT[0:96, 6 + h, :], in_=Mpad2[h][:], identity=identb[:])), reads=["Mpad%d" % h, "identb"], writes=["pT"], signal=(h == 1))
            for h in range(2):
                sc.op("dve", (lambda e, h=h: e.tensor_copy(out=Qx[h][qi][64:96, ti * 128:(ti + 1) * 128], in_=pT[64:96, 6 + h, :])), reads=["pT"], writes=[QXS[h]])

        def attn_group(g, h, bg=None, iters_after=0, pending=None, after_pending=None):
            qi = g % 2
            QX = "Qx%d_%d" % (h, qi)
            nkt = 4 * g + 4

            def qk(kt):
                sp = kt % 2
                sc.op("pe", (lambda e, kt=kt, sp=sp: e.matmul(pS[sp][:], lhsT=Kx[h][0:96, kt * 128:(kt + 1) * 128], rhs=Qx[h][qi][0:96, :], start=True, stop=True)),
                      reads=["Kx%d_%d" % (h, kt // 4), "KxE%d" % h, QX], writes=["pS%d" % sp])

            def ex(kt):
                sp, pi = kt % 2, kt % 3
                PTK = "PT%d" % pi
                sc.op("act", (lambda e: e.activation(out=PT[pi][:], in_=pS[sp][:], func=AF.Exp, scale=0.125)), reads=["pS%d" % sp], writes=[PTK])
                j = kt - 4 * g
                if j >= 0:
                    sc.op("pool", (lambda e: e.tensor_tensor(out=PT[pi][:, 128 * j:128 * j + 128], in0=PT[pi][:, 128 * j:128 * j + 128], in1=trib[:], op=ALU.mult)), reads=[PTK, "trib"], writes=[PTK])
                    if j % 2 == 1:
                        sc.op("pool", (lambda e: e.memset(PT[pi][:, 128 * (j - 1):128 * j], 0.0)), writes=[PTK])

            def pv(kt):
                pi = kt % 3
                sc.op("pe", (lambda e: e.matmul(pO[0:65, :], lhsT=Vx[:, kt, h, 0:65], rhs=PT[pi][:], start=(kt == 0), stop=(kt == nkt - 1))),
                      reads=["Vx", "Vx_%d" % (kt // 4), "PT%d" % pi], writes=["pO"], signal=True)

            qk(0)
            for kt in range(nkt):
                if kt + 1 < nkt:
                    qk(kt + 1)
                if kt == 0 and pending:
                    sc.run_deferred(pending, len(pending))
                    if after_pending is not None:
                        after_pending()
                ex(kt)
                if kt >= 1:
                    pv(kt - 1)
                if bg:
                    sc.run_deferred(bg, 0, iters_left=nkt - kt + iters_after)
            pv(nkt - 1)
            sc.op("dve", (lambda e: e.tensor_copy(out=OT[:], in_=pO[0:65, :])), reads=["pO"], writes=["OT"])
            fin_steps = []
            sc.defer = fin_steps
            for ti in range(4):
                bank, c0, BK = (pb, 384, "pb") if ti % 2 == 0 else (pa, 400, "pa")
                sc.op("pe", (lambda e, ti=ti, bank=bank, c0=c0: e.transpose(out=bank[:, c0:c0 + 65], in_=OT[:, ti * 128:(ti + 1) * 128], identity=identf[0:65, 0:65])), reads=["OT", "identf"], writes=[BK])
                sc.op("dve", (lambda e, ti=ti, bank=bank, c0=c0: e.reciprocal(out=rcp2[:, ti % 2:ti % 2 + 1], in_=bank[:, c0 + 64:c0 + 65])), reads=[BK], writes=["rcp%d" % (ti % 2)])
                sc.op("dve", (lambda e, ti=ti, bank=bank, c0=c0: e.tensor_scalar(out=y_tm[g % 2][:, ti, h * 64:(h + 1) * 64], in0=bank[:, c0:c0 + 64], scalar1=rcp2[:, ti % 2:ti % 2 + 1], scalar2=None, op0=ALU.mult)), reads=[BK, "rcp%d" % (ti % 2)], writes=["y_tm%d" % (g % 2) + "m%d_%d" % (h, ti)])
            sc.defer = None
            return fin_steps

        oneb = sb("oneb", [128, 1], F32)
        sc.op("pool", lambda e: e.memset(oneb[:], 1.0), writes=["oneb"])

        cc_next = [0]

        def emit_cc(k):
            sc.dma("pool", "cc", (lambda e, k=k: e.collective_compute("AllGather", ALU.bypass, replica_groups=[[0, 1, 2, 3], [4, 5, 6, 7]],
                                                                        ins=[ctx["ybuf"][k * CH:(k + 1) * CH, :]], outs=[ctx["yall"][k * 4 * CH:(k + 1) * 4 * CH, :]])),
                   reads=["ych%d" % k, "ccchain"], writes=["yall", "ccchain"], inc=1)

        if fused:
            zt = sb("zt", [128, 256], BF16)
            sc.op("pool", lambda e: e.memset(zt[:], 0.0), writes=["zt"])
            sc.dma("sp", "yzh", lambda e: e.dma_start(out=ctx["ybuf"][0:HALO, :], in_=zt[:]), reads=["zt"], writes=["ych0"])
            for r0 in range(HALO + S1, NCH * CH, 128):
                sc.dma("sp", "yzt", (lambda e, r0=r0: e.dma_start(out=ctx["ybuf"][r0:r0 + 128, :], in_=zt[:])), reads=["zt"], writes=["ych%d" % (NCH - 1)])

        SKIP = os.environ.get("K_SKIP", "").split(",")
        def rec(fn, *a):
            lst = []
            sc.defer = lst
            fn(*a)
            sc.defer = None
            return lst

        def front(g):
            P = [rec(proj_tile, g, ti) for ti in range(4)]
            G = [rec(gla_chunk, g, ti) for ti in range(4)]
            gates = [rec(gate_tile, g, ti) for ti in range(4)] if "gate" not in SKIP else []
            ga = [st for l in gates[0::2] for st in l]
            gb = [st for l in gates[1::2] for st in l]
            return (P[0] + merge_steps(G[0], P[1]) + merge_steps(G[1], P[2]) + merge_steps(G[2], P[3]) + G[3] + ga + gb)

        def store_group(g):
            yt_tokens = ["y_tm%d" % (g % 2) + "g%d" % ti for ti in range(4)] + ["y_tm%d" % (g % 2) + "m%d_%d" % (h, ti) for h in range(2) for ti in range(4)]
            ychs = (["ych%d" % k for k in range((HALO + g * 512) // CH, (HALO + g * 512 + 511) // CH + 1)] if fused else ["y_out"])
            sc.dma("sp", "yst%d" % (g % 2), (lambda e, g=g: e.dma_start(out=y_out[g * 512:(g + 1) * 512, :].rearrange("(a p) c -> p a c", p=128), in_=y_tm[g % 2][:])), reads=yt_tokens, writes=ychs + ["ydone%d" % (g % 2)])
            if fused:
                while cc_next[0] < NCH and HALO + (g + 1) * 512 >= (cc_next[0] + 1) * CH:
                    emit_cc(cc_next[0])
                    cc_next[0] += 1
            for tk in yt_tokens:
                sc.last_w[tk] = sc.last_w["ydone%d" % (g % 2)]
                sc.readers[tk] = []

        wpf = []
        if fused:
            fgm = sb("fgm", [128, 8], F32)
            sc.dma("sp", "kfg", lambda e: e.dma_start(out=fgm[:], in_=ctx["ffn_g"]), writes=["fgm"])
            wsf = [sb("wsf%d" % i, [128, 1024], F32) for i in range(3)]
            wsb = [sb("wsb%d" % i, [128, 1024], BF16) for i in range(3)]
            jobs = [(ctx["w_out"][k * 128:(k + 1) * 128, :], ctx["wout_scr"][:, k, :], 1024, None) for k in range(8)]
            for k in range(8):
                for c0 in range(0, 2 * DFF, 1024):
                    wid = min(1024, 2 * DFF - c0)
                    jobs.append((ctx["w_up"][k * 128:(k + 1) * 128, c0:c0 + wid], ctx["wup_scr"][:, k, c0:c0 + wid], wid, k))
            for fc in range(NFC):
                jobs.append((ctx["w_dn"][fc * 128:(fc + 1) * 128, :], ctx["wdn_scr"][:, fc, :], 1024, None))
            sc.defer = wpf
            prev_out = None
            for j, (src, dst, wid, gk) in enumerate(jobs):
                q = j % 3
                sc.dma("sp", "wpi%d" % q, (lambda e, q=q, src=src, wid=wid: e.dma_start(out=wsf[q][:, 0:wid], in_=src)), writes=["wsf%d" % q])
                if gk is None:
                    sc.op("dve", (lambda e, q=q, wid=wid: e.tensor_copy(out=wsb[q][:, 0:wid], in_=wsf[q][:, 0:wid])), reads=["wsf%d" % q], writes=["wsb%d" % q])
                else:
                    sc.op("dve", (lambda e, q=q, wid=wid, gk=gk: e.tensor_scalar(out=wsb[q][:, 0:wid], in0=wsf[q][:, 0:wid], scalar1=fgm[:, gk:gk + 1], scalar2=None, op0=ALU.mult)),
                          reads=["wsf%d" % q, "fgm"], writes=["wsb%d" % q])
                if prev_out is not None:
                    pq_, pdst, pwid = prev_out
                    sc.dma("sp", "wpo%d" % pq_, (lambda e, q=pq_, dst=pdst, wid=pwid: e.dma_start(out=dst, in_=wsb[q][:, 0:wid])), reads=["wsb%d" % pq_], writes=["wscr"])
                prev_out = (q, dst, wid)
            pq_, pdst, pwid = prev_out
            sc.dma("sp", "wpo%d" % pq_, (lambda e, q=pq_, dst=pdst, wid=pwid: e.dma_start(out=dst, in_=wsb[q][:, 0:wid])), reads=["wsb%d" % pq_], writes=["wscr"])
            sc.defer = None
        wpf_per_group = 3 * (-(-len(wpf) // 3 // max(1, NG1 - 1)))

        pending = None
        pending_store = [None]
        first = front(0)
        sc.run_deferred(first, len(first))
        for g in range(NG1):
            bg = front(g + 1) if g + 1 < NG1 else []
            if wpf:
                take = wpf_per_group if g + 1 < NG1 else len(wpf)
                chunk, wpf[:] = wpf[:take], wpf[take:]
                bg = merge_steps(bg, chunk) if bg else chunk
            for h in range(2):
                fin = attn_group(g, h, bg, (1 - h) * (4 * g + 4), pending, pending_store[0])
                pending_store[0] = None
                pending = fin
                if h == 1:
                    pending_store[0] = (lambda g=g: store_group(g))
            sc.run_deferred(bg, len(bg))
        sc.run_deferred(pending, len(pending))
        pending_store[0]()
        if fused:
            while cc_next[0] < NCH:
                emit_cc(cc_next[0])
                cc_next[0] += 1
        else:
            sc.wait_all("sp", ["y_out"])
        with nc.Block() as block:
            sc.emit(block)
        if fused:
            ctx["pre_waits"] = sc.final_waits()
    return nc


CH = min(1024, NTOK2)
CPS = NTOK2 // CH
NCH = (HALO + S1 + CH - 1) // CH


def build_fused():
    nc = bass.Bass("TRN2", target_bir_lowering=False)
    es_sem = contextlib.ExitStack()
    with es_sem:
        ctx = {"nc": nc, "es_sem": es_sem}
        ctx["xpad"] = nc.dram_tensor("xpad", [HALO + S1, D], F32, kind="ExternalInput").ap()
        ctx["ybuf"] = nc.dram_tensor("ybuf", [NCH * CH, 256], BF16, kind="Internal").ap()
        ctx["yall"] = nc.dram_tensor("yall", [NCH * 4 * CH, 256], BF16, kind="Internal").ap()
        ctx["w_out"] = nc.dram_tensor("w_out", [D, D], F32, kind="ExternalInput").ap()
        ctx["w_up"] = nc.dram_tensor("w_up", [D, 2 * DFF], F32, kind="ExternalInput").ap()
        ctx["w_dn"] = nc.dram_tensor("w_dn", [DFF, D], F32, kind="ExternalInput").ap()
        ctx["ffn_g"] = nc.dram_tensor("ffn_g", [128, 8], F32, kind="ExternalInput").ap()
        ctx["wout_scr"] = nc.dram_tensor("wout_scr", [128, 8, D], BF16, kind="Internal").ap()
        ctx["wup_scr"] = nc.dram_tensor("wup_scr", [128, 8, 2 * DFF], BF16, kind="Internal").ap()
        ctx["wdn_scr"] = nc.dram_tensor("wdn_scr", [128, NFC, D], BF16, kind="Internal").ap()
        build_mixer(ctx)
        build_ffn(ctx)
    return nc


def mixer_inputs(x, attn_norm_g, w_in, w_gate_up, b_gate, gla_norm_g):
    o_mq, o_mk, o_mv, o_gq, o_gk, o_gv, o_gr, o_gg = 0, 512, 1024, 1536, 1792, 2048, 2560, 3072
    pos = np.arange(S1, dtype=np.float32)
    inv_freq = (1.0 / (10000.0 ** (np.arange(0, 64, 2, dtype=np.float32) / np.float32(64)))).astype(np.float32)
    ang = (pos[:, None] * inv_freq[None, :]).astype(np.float32)
    cos = np.cos(ang).astype(np.float32)
    sin = np.sin(ang).astype(np.float32)
    lay = lambda t: np.ascontiguousarray(np.tile(t, (1, 4)).reshape(NT1, 128, 128).transpose(1, 0, 2))
    cos4, sin4 = lay(cos), lay(sin)
    ehot = (np.arange(32)[:, None] == (np.arange(S1)[None, :] // 256)).astype(np.float32).astype(ml_dtypes.bfloat16)
    n = np.arange(32)
    addtab = np.where(n[None, :] < n[:, None], 0.0, np.where(n[None, :] == n[:, None], BIG, -BIG)).astype(np.float32).reshape(1, 32 * 32)
    tri = (np.arange(128)[None, :] >= np.arange(128)[:, None]).astype(np.float32)
    ident = np.eye(128, dtype=np.float32)
    shared = {
        "attn_g": np.ascontiguousarray(attn_norm_g[0].reshape(8, 128).T),
        "cos4": cos4, "sin4": sin4, "ehot": ehot, "addtab": addtab,
        "tri_f": tri, "tri_b": tri.astype(ml_dtypes.bfloat16),
        "ident_b": ident.astype(ml_dtypes.bfloat16), "ident_f": ident,
    }
    W = w_in[0]
    maps = []
    for core in range(8):
        b, j = core // 4, core % 4
        h0, h1 = 2 * j, 2 * j + 1
        cols = np.concatenate([
            np.arange(o_mq + h0 * 64, o_mq + h0 * 64 + 64), np.arange(o_mq + h1 * 64, o_mq + h1 * 64 + 64),
            np.arange(o_mk + h0 * 64, o_mk + h0 * 64 + 64), np.arange(o_mk + h1 * 64, o_mk + h1 * 64 + 64),
            np.arange(o_gq + j * 64, o_gq + j * 64 + 64), np.arange(o_gk + j * 64, o_gk + j * 64 + 64),
            np.arange(o_gg, o_gg + 16),
            np.arange(o_mv + h0 * 64, o_mv + h0 * 64 + 64), np.arange(o_mv + h1 * 64, o_mv + h1 * 64 + 64),
            np.arange(o_gv + j * 128, o_gv + j * 128 + 128), np.arange(o_gr + j * 128, o_gr + j * 128 + 128),
        ])
        m = dict(shared)
        m["x"] = np.ascontiguousarray(x[b, :S1])
        m["w"] = np.ascontiguousarray(W[:, cols])
        m["wg"] = np.ascontiguousarray(w_gate_up[0][:, j * 64:(j + 1) * 64])
        m["bg"] = np.ascontiguousarray(b_gate[0][j * 64:(j + 1) * 64].reshape(1, 64))
        m["gng"] = np.ascontiguousarray(gla_norm_g[0][j].reshape(1, 128))
        maps.append(m)
    return maps


_NC_CACHE = {}


def fused_inputs(x, attn_norm_g, w_in, w_gate_up, b_gate, gla_norm_g, w_out,
                 ffn_norm_g, w_ffn_up, conv_w, conv_b, w_ffn_down, final_norm_g):
    m1 = mixer_inputs(x, attn_norm_g, w_in, w_gate_up, b_gate, gla_norm_g)
    shared2 = {
        "w_out": np.ascontiguousarray(w_out[0]),
        "w_up": np.ascontiguousarray(w_ffn_up[0]),
        "w_dn": np.ascontiguousarray(w_ffn_down[0]),
        "ffn_g": np.ascontiguousarray(ffn_norm_g[0].reshape(8, 128).T),
        "conv_w": np.ascontiguousarray(conv_w[0].reshape(3, 2 * NFC, 128).transpose(2, 0, 1)),
        "conv_b": np.ascontiguousarray(conv_b[0].reshape(2 * NFC, 128).T),
        "final_g": np.ascontiguousarray(final_norm_g.reshape(1, D)),
    }
    xpads = []
    for b in range(B):
        xp = np.zeros((HALO + S1, D), np.float32)
        xp[HALO:] = x[b, :S1]
        xpads.append(xp)
    maps = []
    for core in range(8):
        m = dict(m1[core])
        del m["x"]
        m.update(shared2)
        m["xpad"] = xpads[core // 4]
        c = core % 4
        m["xown"] = np.ascontiguousarray(xpads[core // 4][c * NTOK2:c * NTOK2 + HALO + NTOK2])
        maps.append(m)
    return maps


def kernel_fused(x, attn_norm_g, w_in, w_gate_up, b_gate, gla_norm_g, w_out,
                 ffn_norm_g, w_ffn_up, conv_w, conv_b, w_ffn_down, final_norm_g):
    if "fused" not in _NC_CACHE:
        _NC_CACHE["fused"] = build_fused()
    maps = fused_inputs(x, attn_norm_g, w_in, w_gate_up, b_gate, gla_norm_g, w_out,
                        ffn_norm_g, w_ffn_up, conv_w, conv_b, w_ffn_down, final_norm_g)
    res = run_bass_kernel_spmd(_NC_CACHE["fused"], maps, core_ids=list(range(8)))
    out = np.stack([np.concatenate([np.asarray(res.results[b * 4 + c]["out"]) for c in range(4)], axis=0) for b in range(B)])
    return out.astype(np.float32)


def kernel(x, attn_norm_g, w_in, w_gate_up, b_gate, gla_norm_g, w_out,
           ffn_norm_g, w_ffn_up, conv_w, conv_b, w_ffn_down, final_norm_g):
    args = [np.asarray(a, np.float32) for a in (x, attn_norm_g, w_in, w_gate_up, b_gate, gla_norm_g, w_out,
                                                 ffn_norm_g, w_ffn_up, conv_w, conv_b, w_ffn_down, final_norm_g)]
    return kernel_fused(*args)


def kernel_unfused(x, attn_norm_g, w_in, w_gate_up, b_gate, gla_norm_g, w_out,
                   ffn_norm_g, w_ffn_up, conv_w, conv_b, w_ffn_down, final_norm_g):
    x = np.asarray(x, np.float32)
    args = [np.asarray(a, np.float32) for a in (attn_norm_g, w_in, w_gate_up, b_gate, gla_norm_g, w_out,
                                                 ffn_norm_g, w_ffn_up, conv_w, conv_b, w_ffn_down, final_norm_g)]
    (attn_norm_g, w_in, w_gate_up, b_gate, gla_norm_g, w_out,
     ffn_norm_g, w_ffn_up, conv_w, conv_b, w_ffn_down, final_norm_g) = args
    if "mix" not in _NC_CACHE:
        _NC_CACHE["mix"] = build_mixer()
    maps = mixer_inputs(x, attn_norm_g, w_in, w_gate_up, b_gate, gla_norm_g)
    res = run_bass_kernel_spmd(_NC_CACHE["mix"], maps, core_ids=list(range(8)))
    yT_full = np.zeros((B, D, S), ml_dtypes.bfloat16)
    for core in range(8):
        b, j = core // 4, core % 4
        yc = np.asarray(res.results[core]["y"])
        yT_full[b, 128 * j:128 * j + 128, :] = yc[:, 0:128].T
        yT_full[b, 512 + 128 * j:512 + 128 * j + 128, :] = yc[:, 128:256].T
    if "ffn" not in _NC_CACHE:
        _NC_CACHE["ffn"] = build_ffn()
    maps2 = ffn_inputs(x, yT_full, w_out, w_ffn_up, w_ffn_down, ffn_norm_g, conv_w, conv_b, final_norm_g)
    res2 = run_bass_kernel_spmd(_NC_CACHE["ffn"], maps2, core_ids=list(range(8)))
    out = np.stack([np.concatenate([np.asarray(res2.results[b * 4 + c]["out"]) for c in range(4)], axis=0) for b in range(B)])
    return out.astype(np.float32)
```

```python
import contextlib
import os
import math

import numpy as np
import ml_dtypes

import concourse.bass as bass
import concourse.mybir as mybir
from concourse.bass_utils import run_bass_kernel_spmd

F32 = mybir.dt.float32
BF16 = mybir.dt.bfloat16
ALU = mybir.AluOpType
AF = mybir.ActivationFunctionType
AX = mybir.AxisListType

D = 1024
S = 8192
B = 2
DFF = 2816
NFC = DFF // 128
EPS = 1e-6
NEG = -30000.0


def _units(lst):
    out, cur = [], []
    for st in lst:
        cur.append(st)
        if not st[2].get("hold", False):
            out.append(cur)
            cur = []
    if cur:
        out.append(cur)
    return out


def merge_steps(*lists):
    us = [_units(l) for l in lists if l]
    if not us:
        return []
    tot = [len(u) for u in us]
    pos = [0] * len(us)
    out = []
    while any(p < t for p, t in zip(pos, tot)):
        k = min((i for i in range(len(us)) if pos[i] < tot[i]), key=lambda i: (pos[i] + 0.5) / tot[i])
        out.extend(us[k][pos[k]])
        pos[k] += 1
    return out


class Sched:
    COMPUTE = ("pe", "act", "dve", "pool")

    def __init__(self, nc, es, prefix=""):
        self.nc = nc
        self.es = es
        self.prefix = prefix
        self._cc = {}
        self.defer = None
        self.batch_no = 0
        self.tok_batch = {}
        self.streams = {e: [] for e in ("pe", "act", "dve", "pool", "sp")}
        self.sems = {}
        self.dma_counts = {}
        self.dma_inc = {}
        self.last_w = {}
        self.readers = {}
        self.nsem = 0

    def sem(self, key):
        if key not in self.sems:
            self.sems[key] = self.es.enter_context(self.nc.semaphore(self.prefix + "s%d_%s" % (self.nsem, str(key)[:20].replace(":", "_").replace(" ", ""))))
            self.nsem += 1
        return self.sems[key]

    def _deps(self, reads, writes, eng=None):
        deps = []
        for t in reads:
            if t in self.last_w:
                deps.append(self.last_w[t])
            if t.startswith("p"):
                for r in self.readers.get(t, ()):
                    if not (r[0] == "c" and r[1] == eng):
                        deps.append(r)
        for t in writes:
            if t in self.last_w:
                deps.append(self.last_w[t])
            deps.extend(self.readers.get(t, ()))
        return deps

    def _register(self, ref, reads, writes):
        for t in reads:
            self.readers.setdefault(t, []).append(ref)
        for t in writes:
            self.last_w[t] = ref
            self.readers[t] = []

    def run_deferred(self, lst, n, iters_left=None):
        saved, self.defer = self.defer, None
        done = 0
        hold = False
        if iters_left is None:
            lo = hi = n
        else:
            lo = -(-len(lst) // max(1, iters_left))
            hi = max(4, 2 * lo) if iters_left > 1 else len(lst)
        self.batch_no += 1
        DIST = int(os.environ.get('K_DIST', 1))
        while lst and (done < hi or hold):
            kind, a, kw = lst[0]
            if (iters_left is not None and iters_left > 1 and not hold and kind == "op" and a[0] in ("pe", "act")
                    and any(self.tok_batch.get(r, -99) > self.batch_no - DIST for r in kw["reads"])):
                break
            lst.pop(0)
            kw = dict(kw)
            hold = kw.pop("hold", False)
            (self.op if kind == "op" else self.dma)(*a, **kw)
            for w in kw["writes"]:
                self.tok_batch[w] = self.batch_no
            done += 1
            if iters_left is None and done >= n and not hold:
                break
        self.defer = saved

    def op(self, eng, fn, reads=(), writes=(), signal=True, hold=False):
        if self.defer is not None:
            self.defer.append(("op", (eng, fn), dict(reads=list(reads), writes=list(writes), signal=signal, hold=hold)))
            return
        deps = self._deps(reads, writes, eng)
        st = self.streams[eng]
        idx = len(st)
        st.append(dict(fn=fn, deps=deps, signal=signal, dma=None))
        self._register(("c", eng, idx), reads, writes)

    def dma(self, queue, slot, fn, reads=(), writes=(), inc=16):
        if self.defer is not None:
            self.defer.append(("dma", (queue, slot, fn), dict(reads=list(reads), writes=list(writes), inc=inc)))
            return
        deps = self._deps(reads, writes)
        n = self.dma_counts.get(slot, 0) + 1
        self.dma_counts[slot] = n
        self.dma_inc[slot] = inc
        st = self.streams[queue]
        st.append(dict(fn=fn, deps=deps, signal=False, dma=slot))
        self._register(("d", slot, inc * n), reads, writes)

    def wait_all(self, queue, tokens):
        deps = []
        for t in tokens:
            if t in self.last_w:
                deps.append(self.last_w[t])
        self.streams[queue].append(dict(fn=None, deps=deps, signal=False, dma=None))

    def core_c(self, eng):
        k = id(eng)
        if k not in self._cc:
            self._cc[k] = eng.partition_id() % 4
        return self._cc[k]

    def final_waits(self):
        out = []
        for e in self.COMPUTE:
            n = sum(1 for o in self.streams[e] if o["signal"])
            if n:
                out.append((self.sem(("eng", e)), n))
        for slot, n in self.dma_counts.items():
            out.append((self.sem(("dma", slot)), self.dma_inc[slot] * n))
        return out

    def emit(self, block, pre_waits=()):
        sigval = {}
        for e in self.COMPUTE:
            st = self.streams[e]
            vals = [0] * len(st)
            cnt = 0
            for i, o in enumerate(st):
                if o["signal"]:
                    cnt += 1
                vals[i] = cnt if o["signal"] else None
            nxt = None
            for i in range(len(st) - 1, -1, -1):
                if vals[i] is None:
                    assert nxt is not None or st[i]["dma"] is not None or st[i]["fn"] is None, "last op on %s must signal" % e
                    vals[i] = nxt
                else:
                    nxt = vals[i]
            sigval[e] = vals
        for e in self.COMPUTE:
            self.sem(("eng", e))
        for s in self.dma_counts:
            self.sem(("dma", s))

        def run(ename, eng):
            waited = {}
            for (psem, pval) in pre_waits:
                eng.wait_ge(psem, pval)
            for o in self.streams[ename]:
                need = {}
                for d in o["deps"]:
                    if d[0] == "c":
                        _, e2, idx = d
                        if e2 == ename and ename in ("pe",):
                            continue
                        if e2 == ename and ename == "sp":
                            continue
                        v = sigval[e2][idx]
                        if v is None:
                            continue
                        key = ("eng", e2)
                    else:
                        _, slot, v = d
                        key = ("dma", slot)
                    if v > need.get(key, 0):
                        need[key] = v
                for key, v in need.items():
                    if waited.get(key, 0) >= v:
                        continue
                    eng.wait_ge(self.sems[key], v)
                    waited[key] = v
                if o["fn"] is None:
                    continue
                ins = o["fn"](eng)
                if o["dma"] is not None:
                    ins.then_inc(self.sems[("dma", o["dma"])], self.dma_inc[o["dma"]])
                elif o["signal"]:
                    ins.then_inc(self.sems[("eng", ename)], 1)

        @block.tensor
        def _(eng):
            run("pe", eng)

        @block.scalar
        def _(eng):
            run("act", eng)

        @block.vector
        def _(eng):
            run("dve", eng)

        @block.gpsimd
        def _(eng):
            run("pool", eng)

        @block.sync
        def _(eng):
            run("sp", eng)


def bcast_rows(ap_1xn, nparts):
    return bass.AP(ap_1xn.tensor, ap_1xn.offset, [[0, nparts]] + [list(x) for x in ap_1xn.ap[1:]])


import os
NTOK2 = int(os.environ.get('K_NTOK2', int(os.environ.get('K_S1', 8192)) // 4))
HALO = 128
GW = 256
NG2 = NTOK2 // GW


def build_ffn(ctx=None):
    fused = ctx is not None
    nc = ctx["nc"] if fused else bass.Bass("TRN2", target_bir_lowering=False)
    TT = HALO + NTOK2
    if fused:
        x_in = nc.dram_tensor("xown", [TT, D], F32, kind="ExternalInput").ap()
        yall4 = ctx["yall"].rearrange("(k r t) c -> k t r c", r=4, t=CH)
    else:
        x_in = nc.dram_tensor("x", [TT, D], F32, kind="ExternalInput").ap()
        yT_in = nc.dram_tensor("yT", [D, TT], BF16, kind="ExternalInput").ap()
    if fused:
        w_out_in, w_up_in, w_dn_in, fg_in = ctx["w_out"], ctx["w_up"], ctx["w_dn"], ctx["ffn_g"]
    else:
        w_out_in = nc.dram_tensor("w_out", [D, D], F32, kind="ExternalInput").ap()
        w_up_in = nc.dram_tensor("w_up", [D, 2 * DFF], F32, kind="ExternalInput").ap()
        w_dn_in = nc.dram_tensor("w_dn", [DFF, D], F32, kind="ExternalInput").ap()
        fg_in = nc.dram_tensor("ffn_g", [128, 8], F32, kind="ExternalInput").ap()
    cw_in = nc.dram_tensor("conv_w", [128, 3, 2 * NFC], F32, kind="ExternalInput").ap()
    cb_in = nc.dram_tensor("conv_b", [128, 2 * NFC], F32, kind="ExternalInput").ap()
    og_in = nc.dram_tensor("final_g", [1, D], F32, kind="ExternalInput").ap()
    ident_in = ctx["ident_b"] if fused else nc.dram_tensor("ident", [128, 128], BF16, kind="ExternalInput").ap()
    out = nc.dram_tensor("out", [NTOK2, D], F32, kind="ExternalOutput").ap()
    wdn_scr = ctx["wdn_scr"] if fused else nc.dram_tensor("wdn_scr", [128, NFC, D], BF16, kind="Internal").ap()

    es = contextlib.ExitStack()
    with es:
        sb = lambda name, shape, dt: es.enter_context(nc.sbuf_tensor("f_sb_" + name, shape, dt))
        ps = lambda name, shape, dt: es.enter_context(nc.psum_tensor("f_ps_" + name, shape, dt))
        sc = Sched(nc, ctx["es_sem"] if fused else es, prefix="f_")

        def xrows(e, t0, n):
            return x_in[t0:t0 + n, :]

        _ysl = {}

        def yrows(e, t, n):
            kk, r0 = t // CH, t % CH
            key = (id(e), kk)
            if key not in _ysl:
                _ysl[key] = yall4[bass.ds(sc.core_c(e) * CPS + kk, 1), :, :, :]
            return _ysl[key][0, r0:r0 + n, :, :]

        ytm = [sb("ytm%d" % i, [128, 4, 256], BF16) for i in range(2)] if fused else None
        wout = sb("wout", [128, 8, D], BF16)
        wup = sb("wup", [128, 8, 2 * DFF], BF16)
        stage = [sb("stage%d" % i, [128, 1024], F32) for i in range(2)] if not fused else None
        wdst = [sb("wdst%d" % i, [128, 2, D], BF16) for i in range(3)]
        fg = sb("fg", [128, 8], F32)
        cw = sb("cw", [128, 3, 2 * NFC], F32)
        cb = sb("cb", [128, 2 * NFC], F32)
        ogb = sb("ogb", [128, D], F32)
        ident = sb("ident", [128, 128], BF16)
        xh = [sb("xh%d" % i, [128, 2, D], F32) for i in range(2)]
        yT = [sb("yTs%d" % i, [128, 8, GW], BF16) for i in range(2)]
        hn = [sb("hn%d" % i, [128, D], BF16) for i in range(2)]
        hnT2 = [sb("hnT%d" % i, [128, 8, GW], BF16) for i in range(2)]
        gT2 = [sb("gT%d" % i, [128, NFC, GW], BF16) for i in range(2)]
        uext = [sb("uext%d" % i, [128, 2 + GW], F32) for i in range(4)]
        cbuf = [sb("cbuf%d" % i, [128, GW], F32) for i in range(4)]
        sgb = [sb("sgb%d" % i, [128, GW], F32) for i in range(2)]
        carry = sb("carry", [128, 2 * NFC, 2], F32)
        ssq = sb("ssq", [128, 8], F32)
        rstd = sb("rstd", [128, 8], F32)
        junk = sb("junk", [128, D], BF16)

        p_t = ps("p_t", [128, 8, 128], BF16)
        p_u3 = [ps("p_u%d" % i, [128, 512], F32) for i in range(2)]
        p_h = ps("p_h", [128, 512], F32)
        p_d = [ps("p_d%d" % i, [128, 512], F32) for i in range(4)]
        p_d_all = [[p_d[0], p_d[1]], [p_d[2], p_d[3]]]
        epsb = sb("epsb", [128, 1], F32)
        sc.op("pool", lambda e: e.memset(epsb[:], EPS), writes=["epsb"])

        sc.dma("sp", "c0", lambda e: e.dma_start(out=fg[:], in_=fg_in), writes=["fg"])
        sc.dma("sp", "c1", lambda e: e.dma_start(out=cw[:], in_=cw_in), writes=["cw"])
        sc.dma("sp", "c2", lambda e: e.dma_start(out=cb[:], in_=cb_in), writes=["cb"])
        sc.dma("sp", "c3", lambda e: e.dma_start(out=ogb[:], in_=bcast_rows(og_in, 128)), writes=["ogb"])
        sc.dma("sp", "c4", lambda e: e.dma_start(out=ident[:], in_=ident_in), writes=["ident"])
        sc.op("pool", lambda e: e.memset(carry[:], 0.0), writes=["carry%d" % ch for ch in range(2 * NFC)])

        if fused:
            for k in range(8):
                sc.dma("sp", "wlo%d" % k, (lambda e, k=k: e.dma_start(out=wout[:, k, :], in_=ctx["wout_scr"][:, k, :])), writes=["wout%d" % k])
            for k in range(8):
                sc.dma("sp", "wlu%d" % k, (lambda e, k=k: e.dma_start(out=wup[:, k, :], in_=ctx["wup_scr"][:, k, :])), writes=["wup%d" % k])
        else:
            jobs = []
            for k in range(8):
                jobs.append((w_out_in[k * 128:(k + 1) * 128, :], wout[:, k, :], 1024, None, "wout%d" % k))
            for k in range(8):
                for c0 in range(0, 2 * DFF, 1024):
                    cwid = min(1024, 2 * DFF - c0)
                    jobs.append((w_up_in[k * 128:(k + 1) * 128, c0:c0 + cwid], wup[:, k, c0:c0 + cwid], cwid, k, "wup%d" % k))
            ji = 0
            for (src, dst, wid, gk, wtok) in jobs:
                s = ji % 2
                st = stage[s]
                sc.dma("sp", "stg%d" % s, (lambda e, st=st, src=src, wid=wid: e.dma_start(out=st[:, 0:wid], in_=src)), writes=["stage%d" % s])
                if ji % 2 == 0:
                    if gk is None:
                        sc.op("act", (lambda e, st=st, dst=dst, wid=wid: e.activation(out=dst, in_=st[:, 0:wid], func=AF.Copy)), reads=["stage%d" % s], writes=[wtok])
                    else:
                        sc.op("act", (lambda e, st=st, dst=dst, wid=wid, gk=gk: e.activation(out=dst, in_=st[:, 0:wid], func=AF.Copy, scale=fg[:, gk:gk + 1])),
                              reads=["stage%d" % s, "fg"], writes=[wtok])
                elif gk is None:
                    sc.op("dve", (lambda e, st=st, dst=dst, wid=wid: e.tensor_copy(out=dst, in_=st[:, 0:wid])), reads=["stage%d" % s], writes=[wtok])
                else:
                    sc.op("dve", (lambda e, st=st, dst=dst, wid=wid, gk=gk: e.tensor_scalar(out=dst, in0=st[:, 0:wid], scalar1=fg[:, gk:gk + 1], scalar2=None, op0=ALU.mult)),
                          reads=["stage%d" % s, "fg"], writes=[wtok])
                ji += 1
            for fc in range(NFC):
                s = ji % 2
                st = stage[s]
                sc.dma("sp", "stg%d" % s, (lambda e, st=st, fc=fc: e.dma_start(out=st[:], in_=w_dn_in[fc * 128:(fc + 1) * 128, :])), writes=["stage%d" % s])
                ws = (fc // 2) % 3
                a = fc % 2
                if ji % 2 == 0:
                    sc.op("act", (lambda e, st=st, ws=ws, a=a: e.activation(out=wdst[ws][:, a, :], in_=st[:], func=AF.Copy)), reads=["stage%d" % s], writes=["wdst%d" % ws])
                else:
                    sc.op("dve", (lambda e, st=st, ws=ws, a=a: e.tensor_copy(out=wdst[ws][:, a, :], in_=st[:])), reads=["stage%d" % s], writes=["wdst%d" % ws])
                if a == 1:
                    sc.dma("sp", "wsc%d" % ws, (lambda e, ws=ws, fc=fc: e.dma_start(out=wdn_scr[:, fc - 1:fc + 1, :], in_=wdst[ws][:])), reads=["wdst%d" % ws], writes=["wdn_scr"])
                ji += 1

        groups = [(0, HALO, False)] + [(HALO + g * GW, GW, True) for g in range(NG2)]

        def head(gi, part="all"):
            t0, W, real = groups[gi]
            nt = W // 128
            xb, yb, hT = xh[gi % 2], yT[gi % 2], hnT2[gi % 2]
            XH, YT, HT = "xh%d" % (gi % 2), "yT%d" % (gi % 2), "hnT%d" % (gi % 2)
            if part in ("all", "y"):
                head_y(gi, t0, W, nt, yb, YT)
            if part == "y":
                return
            sc.dma("sp", XH, (lambda e: e.dma_start(out=xb[:, 0:nt, :], in_=xrows(e, t0, nt * 128).rearrange("(a p) d -> p a d", p=128))), writes=[XH])
            head_x(gi, t0, W, nt, xb, yb, hT, XH, YT, HT)

        def head_y(gi, t0, W, nt, yb, YT):
            if not fused:
                sc.dma("sp", YT, (lambda e: e.dma_start(out=yb[:, :, 0:W], in_=yT_in[:, t0:t0 + W].rearrange("(k p) t -> p k t", p=128))), writes=[YT])
            else:
                for ti in range(nt):
                    ys = ytm[ti % 2]
                    YS = "ytm%d" % (ti % 2)
                    sc.dma("sp", YS, (lambda e, ys=ys, ti=ti: e.dma_start(out=ys[:], in_=yrows(e, t0 + ti * 128, 128))), writes=[YS])
                    for kk in range(8):
                        src = ys[:, kk, 0:128] if kk < 4 else ys[:, kk - 4, 128:256]
                        sc.op("pe", (lambda e, src=src, kk=kk: e.transpose(out=p_t[:, kk, :], in_=src, identity=ident[:])), reads=[YS, "ident"], writes=["p_t"], signal=(kk == 7))
                    sc.op("dve", (lambda e, ti=ti: e.tensor_copy(out=yb[:, :, ti * 128:(ti + 1) * 128], in_=p_t[:])), reads=["p_t"], writes=[YT])

        def head_x(gi, t0, W, nt, xb, yb, hT, XH, YT, HT):
            for ti in range(nt):
                for half in range(2):
                    ph = p_h
                    for k in range(8):
                        sc.op("pe", (lambda e, ph=ph, k=k, ti=ti, half=half: e.matmul(ph[:], lhsT=yb[:, k, ti * 128:(ti + 1) * 128], rhs=wout[:, k, half * 512:(half + 1) * 512], start=(k == 0), stop=(k == 7))),
                              reads=[YT, "wout%d" % k], writes=["p_h"], signal=(k == 7), hold=(k != 7))
                    sc.op("dve", (lambda e, ph=ph, ti=ti, half=half: e.tensor_tensor(out=xb[:, ti, half * 512:(half + 1) * 512], in0=ph[:], in1=xb[:, ti, half * 512:(half + 1) * 512], op=ALU.add)),
                          reads=["p_h"], writes=[XH])
                col = (gi % 2) * 2 + ti
                sc.op("act", (lambda e, ti=ti, col=col: e.activation(out=junk[:], in_=xb[:, ti, :], func=AF.Square, accum_out=ssq[:, col:col + 1])), reads=[XH], writes=["junk", "ssq%d" % col])
                sc.op("act", (lambda e, col=col: e.activation(out=rstd[:, col:col + 1], in_=ssq[:, col:col + 1], func=AF.Sqrt, scale=1.0 / D, bias=epsb[:, 0:1])), reads=["ssq%d" % col, "epsb"], writes=["rstd%d" % col])
                sc.op("dve", (lambda e, col=col: e.reciprocal(out=rstd[:, col:col + 1], in_=rstd[:, col:col + 1])), reads=["rstd%d" % col], writes=["rstd%d" % col])
                hb = hn[ti % 2]
                HN = "hn%d" % (ti % 2)
                sc.op("dve", (lambda e, hb=hb, ti=ti, col=col: e.tensor_scalar(out=hb[:], in0=xb[:, ti, :], scalar1=rstd[:, col:col + 1], scalar2=None, op0=ALU.mult)), reads=[XH, "rstd%d" % col], writes=[HN])
                for k in range(8):
                    sc.op("pe", (lambda e, hb=hb, k=k: e.transpose(out=p_t[:, k, :], in_=hb[:, k * 128:(k + 1) * 128], identity=ident[:])), reads=[HN, "ident"], writes=["p_t"], signal=(k == 7))
                sc.op("dve", (lambda e, ti=ti: e.tensor_copy(out=hT[:, :, ti * 128:(ti + 1) * 128], in_=p_t[:])), reads=["p_t"], writes=[HT])

        HW = 32

        def up(gi, bg):
            t0, W, real = groups[gi]
            hT, HT = hnT2[gi % 2], "hnT%d" % (gi % 2)
            pend = None
            for fc in range(NFC + 1):
                if fc < NFC:
                    cs = []
                    for hv in range(2):
                        ch = hv * NFC + fc
                        pi = (2 * fc + hv) % 4
                        pq = (2 * fc + hv) % 2
                        pu = p_u3[pq]
                        for k in range(8):
                            if real:
                                sc.op("pe", (lambda e, pu=pu, k=k, ch=ch: e.matmul(pu[:, 0:W], lhsT=wup[:, k, ch * 128:(ch + 1) * 128], rhs=hT[:, k, 0:W], start=(k == 0), stop=(k == 7))),
                                      reads=[HT, "wup%d" % k], writes=["p_u%d" % pq], signal=(k == 7))
                            else:
                                sc.op("pe", (lambda e, pu=pu, k=k, ch=ch: e.matmul(pu[:, 0:HW], lhsT=wup[:, k, ch * 128:(ch + 1) * 128], rhs=hT[:, k, W - HW:W], start=(k == 0), stop=(k == 7))),
                                      reads=[HT, "wup%d" % k], writes=["p_u%d" % pq], signal=(k == 7))
                        ub, UB = uext[pi], "uext%d" % pi
                        cbf, CB = cbuf[pi], "cbuf%d" % pi
                        if not real:
                            sc.op("act", (lambda e, ub=ub, pu=pu: e.activation(out=ub[:, 2:2 + HW], in_=pu[:, 0:HW], func=AF.Copy)), reads=["p_u%d" % pq], writes=[UB])
                            sc.op("pool", (lambda e, ub=ub, ch=ch: e.tensor_copy(out=carry[:, ch, :], in_=ub[:, HW:HW + 2])), reads=[UB], writes=["carry%d" % ch])
                            cs.append((cbf, CB))
                            continue
                        sc.op("act", (lambda e, ub=ub, pu=pu: e.activation(out=ub[:, 2:2 + W], in_=pu[:, 0:W], func=AF.Copy)), reads=["p_u%d" % pq], writes=[UB])
                        sc.op("pool", (lambda e, ub=ub, ch=ch: e.tensor_copy(out=ub[:, 0:2], in_=carry[:, ch, :])), reads=["carry%d" % ch], writes=[UB])
                        sc.op("act", (lambda e, cbf=cbf, pu=pu, ch=ch: e.activation(out=cbf[:, 0:W], in_=pu[:, 0:W], func=AF.Identity, scale=cw[:, 2, ch:ch + 1], bias=cb[:, ch:ch + 1])), reads=["p_u%d" % pq, "cw", "cb"], writes=[CB])
                        sc.op("pool", (lambda e, ub=ub, ch=ch: e.tensor_copy(out=carry[:, ch, :], in_=ub[:, W:W + 2])), reads=[UB], writes=["carry%d" % ch])
                        sc.op("dve", (lambda e, cbf=cbf, ub=ub, ch=ch: e.scalar_tensor_tensor(out=cbf[:, 0:W], in0=ub[:, 1:1 + W], scalar=cw[:, 1, ch:ch + 1], in1=cbf[:, 0:W], op0=ALU.mult, op1=ALU.add)), reads=[UB, CB, "cw"], writes=[CB])
                        sc.op("dve", (lambda e, cbf=cbf, ub=ub, ch=ch: e.scalar_tensor_tensor(out=cbf[:, 0:W], in0=ub[:, 0:W], scalar=cw[:, 0, ch:ch + 1], in1=cbf[:, 0:W], op0=ALU.mult, op1=ALU.add)), reads=[UB, CB, "cw"], writes=[CB])
                        cs.append((cbf, CB))
                if pend is not None and real:
                    pfc, pcs = pend
                    sg = sgb[pfc % 2]
                    SG = "sgb%d" % (pfc % 2)
                    sc.op("act", (lambda e, sg=sg, c0=pcs[0][0]: e.activation(out=sg[:, 0:W], in_=c0[:, 0:W], func=AF.Silu)), reads=[pcs[0][1]], writes=[SG])
                    sc.op("pool", (lambda e, sg=sg, c1=pcs[1][0], pfc=pfc: e.tensor_tensor(out=gT2[gi % 2][:, pfc, 0:W], in0=sg[:, 0:W], in1=c1[:, 0:W], op=ALU.mult)), reads=[SG, pcs[1][1]], writes=["gT%d" % (gi % 2)])
                pend = (fc, cs) if fc < NFC else None
                if bg:
                    sc.run_deferred(bg, 0, iters_left=NFC + 1 - fc)

        def tail(gi):
            t0, W, real = groups[gi]
            nt = W // 128
            xb, XH = xh[gi % 2], "xh%d" % (gi % 2)
            for fc in range(0, NFC, 2):
                ws = (fc // 2) % 3
                sc.dma("sp", "wdl%d" % ws, (lambda e, ws=ws, fc=fc: e.dma_start(out=wdst[ws][:], in_=wdn_scr[:, fc:fc + 2, :])), reads=["wdn_scr"], writes=["wdst%d" % ws])
                for ti in range(nt):
                    for half in range(2):
                        pd = p_d_all[ti][half]
                        for a in range(2):
                            sc.op("pe", (lambda e, pd=pd, ws=ws, a=a, fc=fc, ti=ti, half=half: e.matmul(pd[:], lhsT=gT2[gi % 2][:, fc + a, ti * 128:(ti + 1) * 128], rhs=wdst[ws][:, a, half * 512:(half + 1) * 512], start=(fc + a == 0), stop=(fc + a == NFC - 1))),
                                  reads=["gT%d" % (gi % 2), "wdst%d" % ws], writes=["p_d%d_%d" % (ti, half)], signal=True)
            for ti in range(nt):
                for half in range(2):
                    pd = p_d_all[ti][half]
                    sc.op("dve", (lambda e, pd=pd, ti=ti, half=half: e.tensor_tensor(out=xb[:, ti, half * 512:(half + 1) * 512], in0=pd[:], in1=xb[:, ti, half * 512:(half + 1) * 512], op=ALU.add)),
                          reads=["p_d%d_%d" % (ti, half)], writes=[XH])
                col = 4 + ti
                sc.op("act", (lambda e, ti=ti, col=col: e.activation(out=junk[:], in_=xb[:, ti, :], func=AF.Square, accum_out=ssq[:, col:col + 1])), reads=[XH], writes=["junk", "ssq%d" % col])
                sc.op("act", (lambda e, col=col: e.activation(out=rstd[:, col:col + 1], in_=ssq[:, col:col + 1], func=AF.Sqrt, scale=1.0 / D, bias=epsb[:, 0:1])), reads=["ssq%d" % col, "epsb"], writes=["rstd%d" % col])
                sc.op("dve", (lambda e, col=col: e.reciprocal(out=rstd[:, col:col + 1], in_=rstd[:, col:col + 1])), reads=["rstd%d" % col], writes=["rstd%d" % col])
                sc.op("dve", (lambda e, ti=ti, col=col: e.scalar_tensor_tensor(out=xb[:, ti, :], in0=xb[:, ti, :], scalar=rstd[:, col:col + 1], in1=ogb[:], op0=ALU.mult, op1=ALU.mult)), reads=["rstd%d" % col, "ogb"], writes=[XH])
            r0 = t0 - HALO
            sc.dma("sp", "ost%d" % (gi % 2), (lambda e, r0=r0: e.dma_start(out=out[r0:r0 + nt * 128, :].rearrange("(a p) d -> p a d", p=128), in_=xb[:, 0:nt, :])), reads=[XH], writes=["out"])

        head(0)
        for gi in range(len(groups)):
            def rec2(fn, *a):
                lst = []
                sc.defer = lst
                fn(*a)
                sc.defer = None
                return lst
            bg = ((rec2(head, gi + 1, "y") if gi + 1 < len(groups) else [])
                  + (rec2(tail, gi - 1) if (gi >= 1 and groups[gi - 1][2]) else [])
                  + (rec2(head, gi + 1, "x") if gi + 1 < len(groups) else []))
            up(gi, bg)
            sc.run_deferred(bg, len(bg))
        tail(len(groups) - 1)
        sc.wait_all("sp", ["out"])
        with nc.Block() as block:
            sc.emit(block, pre_waits=(ctx["pre_waits"] if fused else ()))
    return nc


def _bf16(a):
    return a.astype(ml_dtypes.bfloat16)


def ffn_inputs(x, yT_full, w_out, w_ffn_up, w_ffn_down, ffn_norm_g, conv_w, conv_b, final_norm_g):
    ident = np.eye(128, dtype=np.float32).astype(ml_dtypes.bfloat16)
    shared = {
        "w_out": np.ascontiguousarray(w_out[0]),
        "w_up": np.ascontiguousarray(w_ffn_up[0]),
        "w_dn": np.ascontiguousarray(w_ffn_down[0]),
        "ffn_g": np.ascontiguousarray(ffn_norm_g[0].reshape(8, 128).T),
        "conv_w": np.ascontiguousarray(conv_w[0].reshape(3, 2 * NFC, 128).transpose(2, 0, 1)),
        "conv_b": np.ascontiguousarray(conv_b[0].reshape(2 * NFC, 128).T),
        "final_g": np.ascontiguousarray(final_norm_g.reshape(1, D)),
        "ident": ident,
    }
    maps = []
    for core in range(8):
        b, c = core // 4, core % 4
        t0 = c * NTOK2
        xs = np.zeros((HALO + NTOK2, D), np.float32)
        ys = np.zeros((D, HALO + NTOK2), ml_dtypes.bfloat16)
        xs[HALO:] = x[b, t0:t0 + NTOK2]
        ys[:, HALO:] = yT_full[b][:, t0:t0 + NTOK2]
        if c > 0:
            xs[:HALO] = x[b, t0 - HALO:t0]
            ys[:, :HALO] = yT_full[b][:, t0 - HALO:t0]
        m = dict(shared)
        m["x"] = xs
        m["yT"] = ys
        maps.append(m)
    return maps


S1 = int(os.environ.get('K_S1', S))
NT1 = S1 // 128
NG1 = S1 // 512
NCA = 400
NCB = 384
BIG = 1.0e30


def build_mixer(ctx=None):
    fused = ctx is not None
    nc = ctx["nc"] if fused else bass.Bass("TRN2", target_bir_lowering=False)
    if fused:
        x_in = ctx["xpad"][HALO:HALO + S1, :]
    else:
        x_in = nc.dram_tensor("x", [S1, D], F32, kind="ExternalInput").ap()
    ag_in = nc.dram_tensor("attn_g", [128, 8], F32, kind="ExternalInput").ap()
    w_in = nc.dram_tensor("w", [D, NCA + NCB], F32, kind="ExternalInput").ap()
    wg_in = nc.dram_tensor("wg", [16, 64], F32, kind="ExternalInput").ap()
    bg_in = nc.dram_tensor("bg", [1, 64], F32, kind="ExternalInput").ap()
    gng_in = nc.dram_tensor("gng", [1, 128], F32, kind="ExternalInput").ap()
    cos_in = nc.dram_tensor("cos4", [128, NT1, 128], F32, kind="ExternalInput").ap()
    sin_in = nc.dram_tensor("sin4", [128, NT1, 128], F32, kind="ExternalInput").ap()
    ehot_in = nc.dram_tensor("ehot", [32, S1], BF16, kind="ExternalInput").ap()
    addtab_in = nc.dram_tensor("addtab", [1, 32 * 32], F32, kind="ExternalInput").ap()
    trif_in = nc.dram_tensor("tri_f", [128, 128], F32, kind="ExternalInput").ap()
    trib_in = nc.dram_tensor("tri_b", [128, 128], BF16, kind="ExternalInput").ap()
    identb_in = nc.dram_tensor("ident_b", [128, 128], BF16, kind="ExternalInput").ap()
    identf_in = nc.dram_tensor("ident_f", [128, 128], F32, kind="ExternalInput").ap()
    if fused:
        ctx["ident_b"] = identb_in
        y_out = ctx["ybuf"][HALO:HALO + S1, :]
    else:
        y_out = nc.dram_tensor("y", [S1, 256], BF16, kind="ExternalOutput").ap()

    es = contextlib.ExitStack()
    with es:
        sb = lambda name, shape, dt: es.enter_context(nc.sbuf_tensor("m_sb_" + name, shape, dt))
        ps = lambda name, shape, dt: es.enter_context(nc.psum_tensor("m_ps_" + name, shape, dt))
        sc = Sched(nc, ctx["es_sem"] if fused else es, prefix="m_")

        Kx = [sb("Kx%d" % h, [96, S1], BF16) for h in range(2)]
        Vx = sb("Vx", [128, NT1, 2, 72], BF16)
        kms = [sb("kms%d" % h, [64, 32], BF16) for h in range(2)]
        kmsf = [sb("kmsf%d" % h, [64, 32], F32) for h in range(2)]
        wbf = sb("wbf", [128, 8, NCA + NCB], BF16)
        wstage = [sb("wstage%d" % i, [128, NCA + NCB], F32) for i in range(2)]
        ag = sb("ag", [128, 8], F32)
        wgf = sb("wgf", [16, 64], F32)
        wgb = sb("wgb", [16, 64], BF16)
        bgb = sb("bgb", [128, 64], F32)
        gngb = sb("gngb", [128, 128], F32)
        addtab = sb("addtab", [128, 32, 32], F32)
        trif = sb("trif", [128, 128], F32)
        trib = sb("trib", [128, 128], BF16)
        identb = sb("identb", [128, 128], BF16)
        identf = sb("identf", [128, 128], F32)
        epsb = sb("epsb", [128, 1], F32)
        Sst = sb("Sst", [64, 128], F32)
        Sbf = sb("Sbf", [64, 128], BF16)
        xt = [sb("xt%d" % i, [128, D], F32) for i in range(2)]
        xnb = [sb("xnb%d" % i, [128, D], BF16) for i in range(2)]
        xnT = [sb("xnT%d" % i, [128, 8, 128], BF16) for i in range(2)]
        junk = sb("junk", [128, D], BF16)
        st1 = sb("st1", [128, 16], F32)
        cos4 = [sb("cos4_%d" % i, [128, 4, 128], F32) for i in range(2)]
        sin4 = [sb("sin4_%d" % i, [128, 4, 128], F32) for i in range(2)]
        rtmp = [sb("rtmp%d" % i, [128, 128], F32) for i in range(4)]
        qk_tm = [sb("qk_tm%d" % i, [128, 256], BF16) for i in range(2)]
        g_tm = [sb("g_tm%d" % i, [128, 144], BF16) for i in range(2)]
        gv_tm = [sb("gv_tm%d" % i, [128, 128], BF16) for i in range(2)]
        sgr = [sb("sgr%d" % i, [128, 128], F32) for i in range(2)]
        sgt = [sb("sgt%d" % i, [128, 128], F32) for i in range(2)]
        Qx = [[sb("Qx%d_%d" % (h, i), [96, 512], BF16) for i in range(2)] for h in range(2)]
        glrT2 = [sb("glrT%d" % i, [16, 128], BF16) for i in range(2)]
        gqk = [sb("gqk%d" % i, [64, 2, 128], BF16) for i in range(2)]
        zb = sb("zb", [128, 64], F32)
        lsp = sb("lsp", [128, 64], F32)
        eG = sb("eG", [64, 128], F32)
        eGi = sb("eGi", [64, 128], F32)
        qtT = sb("qtT", [64, 128], BF16)
        ktT = sb("ktT", [64, 128], BF16)
        kt_tm = sb("kt_tm", [128, 64], BF16)
        AT = sb("AT", [128, 128], BF16)
        stmp = sb("stmp", [64, 128], F32)
        ogt = sb("ogt", [128, 128], F32)
        gm2 = sb("gm2", [128, 2, 32], F32)
        m82 = sb("m82", [128, 2, 8], F32)
        thr2 = sb("thr2", [128, 2], F32)
        Mpad2 = [sb("Mpad%d" % h, [128, 96], BF16) for h in range(2)]
        PT = [sb("PT%d" % i, [128, 512], BF16) for i in range(3)]
        OT = sb("OT", [65, 512], F32)
        rcp = sb("rcp", [128, 1], F32)
        rcp2 = sb("rcp2", [128, 2], F32)
        y_tm = [sb("y_tm%d" % i, [128, 4, 256], BF16) for i in range(2)]

        pa = ps("pa", [128, 512], F32)
        pb = ps("pb", [128, 512], F32)
        pT = ps("pT", [128, 8, 128], BF16)
        pxT = ps("pxT", [128, 8, 128], BF16)
        pS = [ps("pS%d" % i, [128, 512], F32) for i in range(2)]
        pO = ps("pO", [128, 512], F32)
        pM = ps("pM", [128, 512], F32)
        pMb = pM

        cl = [("ag", ag, ag_in), ("wgf", wgf, wg_in), ("trif", trif, trif_in), ("trib", trib, trib_in),
              ("identb", identb, identb_in), ("identf", identf, identf_in)]
        for i, (tk, t, src) in enumerate(cl):
            sc.dma("sp", "k%d" % i, (lambda e, t=t, src=src: e.dma_start(out=t[:], in_=src)), writes=[tk])
        sc.dma("sp", "kb0", lambda e: e.dma_start(out=bgb[:], in_=bcast_rows(bg_in, 128)), writes=["bgb"])
        sc.dma("sp", "kb1", lambda e: e.dma_start(out=gngb[:], in_=bcast_rows(gng_in, 128)), writes=["gngb"])
        sc.dma("sp", "kb2", lambda e: e.dma_start(out=addtab[:].rearrange("p a b -> p (a b)"), in_=bcast_rows(addtab_in, 128)), writes=["addtab"])
        for h in range(2):
            sc.dma("sp", "ke%d" % h, (lambda e, h=h: e.dma_start(out=Kx[h][64:96, :], in_=ehot_in)), writes=["KxE%d" % h])
        sc.op("pool", lambda e: e.memset(epsb[:], EPS), writes=["epsb"])
        sc.op("pool", lambda e: e.memset(Vx[:].rearrange("p a b c -> p (a b c)"), 1.0), writes=["Vx"])
        for h in range(2):
            sc.op("pool", (lambda e, h=h: e.memset(kms[h][:], 0.0)), writes=["kms%d" % h])
            for i in range(2):
                sc.op("pool", (lambda e, h=h, i=i: e.memset(Qx[h][i][:], 0.0)), writes=["Qx%d_%d" % (h, i)])
        for h in range(2):
            sc.op("pool", (lambda e, h=h: e.memset(Mpad2[h][:], 0.0)), writes=["Mpad%d" % h])
        sc.op("pool", lambda e: e.memset(Sst[:], 0.0), writes=["Sst"])
        sc.op("pool", lambda e: e.memset(Sbf[:], 0.0), writes=["Sbf"])
        sc.op("dve", lambda e: e.tensor_copy(out=wgb[:], in_=wgf[:]), reads=["wgf"], writes=["wgb"])
        for k in range(8):
            s = k % 2
            sc.dma("sp", "ws%d" % s, (lambda e, s=s, k=k: e.dma_start(out=wstage[s][:], in_=w_in[k * 128:(k + 1) * 128, :])), writes=["wstage%d" % s])
            if k % 2:
                sc.op("act", (lambda e, s=s, k=k: e.activation(out=wbf[:, k, :], in_=wstage[s][:], func=AF.Copy, scale=ag[:, k:k + 1])), reads=["wstage%d" % s, "ag"], writes=["wbf%d" % k])
            else:
                sc.op("dve", (lambda e, s=s, k=k: e.tensor_scalar(out=wbf[:, k, :], in0=wstage[s][:], scalar1=ag[:, k:k + 1], scalar2=None, op0=ALU.mult)),
                      reads=["wstage%d" % s, "ag"], writes=["wbf%d" % k])

        STOP = int(os.environ.get("K_STOP", 99))

        def proj_tile(g, ti):
            T = g * 4 + ti
            s2 = T % 2
            XT, XNB, XNT = "xt%d" % s2, "xnb%d" % s2, "xnT%d" % s2
            c0, c1 = (T % 4) * 2, (T % 4) * 2 + 1
            sc.dma("sp", XT, (lambda e: e.dma_start(out=xt[s2][:], in_=x_in[T * 128:(T + 1) * 128, :])), writes=[XT])
            sc.op("act", (lambda e: e.activation(out=junk[:], in_=xt[s2][:], func=AF.Square, accum_out=st1[:, c0:c0 + 1])), reads=[XT], writes=["junk", "st1_%d" % c0])
            sc.op("act", (lambda e: e.activation(out=st1[:, c1:c1 + 1], in_=st1[:, c0:c0 + 1], func=AF.Ln, scale=1.0 / D, bias=epsb[:, 0:1])), reads=["st1_%d" % c0, "epsb"], writes=["st1_%d" % c1])
            sc.op("act", (lambda e: e.activation(out=st1[:, c1:c1 + 1], in_=st1[:, c1:c1 + 1], func=AF.Exp, scale=-0.5)), reads=["st1_%d" % c1], writes=["st1_%d" % c1])
            sc.op("dve", (lambda e: e.tensor_scalar(out=xnb[s2][:], in0=xt[s2][:], scalar1=st1[:, c1:c1 + 1], scalar2=None, op0=ALU.mult)), reads=[XT, "st1_%d" % c1], writes=[XNB])
            for k in range(8):
                sc.op("pe", (lambda e, k=k: e.transpose(out=pxT[:, k, :], in_=xnb[s2][:, k * 128:(k + 1) * 128], identity=identb[:])), reads=[XNB, "identb"], writes=["pxT"], signal=(k == 7))
            sc.op("dve", (lambda e: e.tensor_copy(out=xnT[s2][:], in_=pxT[:])), reads=["pxT"], writes=[XNT])
            if STOP < 2:
                return
            for k in range(8):
                sc.op("pe", (lambda e, k=k: e.matmul(pa[:, 0:NCA], lhsT=xnT[s2][:, k, :], rhs=wbf[:, k, 0:NCA], start=(k == 0), stop=(k == 7))), reads=[XNT, "wbf%d" % k], writes=["pa"], signal=(k == 7), hold=(k != 7))
            for k in range(8):
                sc.op("pe", (lambda e, k=k: e.matmul(pb[:, 0:NCB], lhsT=xnT[s2][:, k, :], rhs=wbf[:, k, NCA:NCA + NCB], start=(k == 0), stop=(k == 7))), reads=[XNT, "wbf%d" % k], writes=["pb"], signal=(k == 7), hold=(k != 7))
            if STOP < 3:
                return
            gs = g % 2
            if ti == 0:
                sc.dma("sp", "cos%d" % gs, (lambda e: e.dma_start(out=cos4[gs][:], in_=cos_in[:, g * 4:(g + 1) * 4, :])), writes=["cos%d" % gs])
                sc.dma("sp", "sin%d" % gs, (lambda e: e.dma_start(out=sin4[gs][:], in_=sin_in[:, g * 4:(g + 1) * 4, :])), writes=["sin%d" % gs])
            pav = pa[:, 0:256].rearrange("p (a b c) -> p a b c", a=4, b=2)
            cv = cos4[gs][:, ti, :].rearrange("p (a c) -> p a c", a=4)
            sv = sin4[gs][:, ti, :].rearrange("p (a c) -> p a c", a=4)
            qkv = qk_tm[s2][:].rearrange("p (a b c) -> p a b c", a=4, b=2)
            rv = [rtmp[i][:].rearrange("p (a c) -> p a c", a=4) for i in range(4)]
            CS = ["cos%d" % gs, "sin%d" % gs]
            QK = "qk_tm%d" % s2
            sc.op("dve", (lambda e: e.tensor_tensor(out=rv[0], in0=pav[:, :, 0, :], in1=cv, op=ALU.mult)), reads=["pa"] + CS, writes=["rtmp0"])
            sc.op("dve", (lambda e: e.tensor_tensor(out=rv[1], in0=pav[:, :, 1, :], in1=sv, op=ALU.mult)), reads=["pa"] + CS, writes=["rtmp1"])
            sc.op("dve", (lambda e: e.tensor_tensor(out=qkv[:, :, 0, :], in0=rv[0], in1=rv[1], op=ALU.subtract)), reads=["rtmp0", "rtmp1"], writes=[QK + "a"])
            sc.op("dve", (lambda e: e.tensor_tensor(out=rv[2], in0=pav[:, :, 1, :], in1=cv, op=ALU.mult)), reads=["pa"] + CS, writes=["rtmp2"])
            sc.op("dve", (lambda e: e.tensor_tensor(out=rv[3], in0=pav[:, :, 0, :], in1=sv, op=ALU.mult)), reads=["pa"] + CS, writes=["rtmp3"])
            sc.op("dve", (lambda e: e.tensor_tensor(out=qkv[:, :, 1, :], in0=rv[2], in1=rv[3], op=ALU.add)), reads=["rtmp2", "rtmp3"], writes=[QK + "b"])
            if STOP < 4:
                return
            GT = "g_tm%d" % s2
            sc.op("dve", (lambda e: e.tensor_copy(out=g_tm[s2][:], in_=pa[:, 256:400])), reads=["pa"], writes=[GT])
            sc.op("dve", (lambda e: e.tensor_copy(out=Vx[:, T, :, 0:64], in_=pb[:, 0:128].rearrange("p (h d) -> p h d", h=2))), reads=["pb", "Vx"], writes=["Vx_%d" % g])
            GV = "gv_tm%d" % s2
            sc.op("dve", (lambda e: e.tensor_copy(out=gv_tm[s2][:], in_=pb[:, 128:256])), reads=["pb"], writes=[GV])
            SGT, SGR = "sgt%d" % s2, "sgr%d" % s2
            sc.op("act", (lambda e: e.activation(out=sgt[s2][:], in_=pb[:, 256:384], func=AF.Exp, scale=-1.0)), reads=["pb"], writes=[SGT])
            sc.op("dve", (lambda e: e.tensor_copy(out=sgr[s2][:], in_=pb[:, 256:384])), reads=["pb"], writes=[SGR])
            if STOP < 5:
                return
            for i in range(4):
                sc.op("pe", (lambda e, i=i: e.transpose(out=pT[0:64, i, :], in_=qk_tm[s2][:, i * 64:(i + 1) * 64], identity=identb[:])), reads=[QK + "a", QK + "b", "identb"], writes=["pT"], signal=False)
            for i in range(2):
                sc.op("pe", (lambda e, i=i: e.transpose(out=pT[0:64, 4 + i, :], in_=g_tm[s2][:, i * 64:(i + 1) * 64], identity=identb[:])), reads=[GT, "identb"], writes=["pT"], signal=False)
            sc.op("pe", (lambda e: e.transpose(out=pT[0:16, 6, :], in_=g_tm[s2][:, 128:144], identity=identb[:])), reads=[GT, "identb"], writes=["pT"], signal=True)
            qi = g % 2
            sc.op("dve", (lambda e: e.tensor_copy(out=glrT2[s2][:], in_=pT[0:16, 6, :])), reads=["pT"], writes=["glrT%d" % s2])
            sc.op("dve", (lambda e: e.tensor_copy(out=gqk[s2][:], in_=pT[0:64, 4:6, :])), reads=["pT"], writes=["gqk%d" % s2])
            for h in range(2):
                sc.op("dve", (lambda e, h=h: e.tensor_copy(out=Qx[h][qi][0:64, ti * 128:(ti + 1) * 128], in_=pT[0:64, h, :])),
                      reads=["pT"], writes=["Qx%d_%d" % (h, qi)])
                sc.op("dve", (lambda e, h=h: e.tensor_copy(out=Kx[h][0:64, T * 128:(T + 1) * 128], in_=pT[0:64, 2 + h, :])),
                      reads=["pT"], writes=["Kx%d_%d" % (h, g)])
            if STOP < 6:
                return
            if T % 2 == 1:
                n = T // 2
                for h in range(2):
                    sc.op("dve", (lambda e, h=h: e.tensor_reduce(out=kmsf[h][:, n:n + 1], in_=Kx[h][0:64, n * 256:(n + 1) * 256], axis=AX.X, op=ALU.add)), reads=["Kx%d_%d" % (h, g)], writes=["kmsf%d" % h])
                    sc.op("dve", (lambda e, h=h: e.tensor_copy(out=kms[h][:, n:n + 1], in_=kmsf[h][:, n:n + 1])), reads=["kmsf%d" % h], writes=["kms%d" % h])

        def gla_chunk(g, ti):
            T = g * 4 + ti
            s2 = T % 2
            GV, SGR = "gv_tm%d" % s2, "sgr%d" % s2
            sc.op("pe", (lambda e: e.matmul(pM[:, 0:64], lhsT=glrT2[s2][:], rhs=wgb[:], start=True, stop=True)), reads=["glrT%d" % s2, "wgb"], writes=["pM"])
            sc.op("dve", (lambda e: e.tensor_tensor(out=zb[:], in0=pM[:, 0:64], in1=bgb[:], op=ALU.add)), reads=["pM", "bgb"], writes=["zb"])
            sc.op("act", (lambda e: e.activation(out=zb[:], in_=zb[:], func=AF.Exp, scale=-1.0)), reads=["zb"], writes=["zb"])
            sc.op("act", (lambda e: e.activation(out=lsp[:], in_=zb[:], func=AF.Ln, bias=oneb[:, 0:1])), reads=["zb", "oneb"], writes=["lsp"])
            sc.op("pe", (lambda e: e.matmul(pM[0:64, 128:256], lhsT=lsp[:], rhs=trif[:], start=True, stop=True)), reads=["lsp", "trif"], writes=["pM"])
            sc.op("act", (lambda e: e.activation(out=eG[:], in_=pM[0:64, 128:256], func=AF.Exp, scale=-1.0 / 16.0)), reads=["pM"], writes=["eG"])
            sc.op("act", (lambda e: e.activation(out=eGi[:], in_=pM[0:64, 128:256], func=AF.Exp, scale=1.0 / 16.0)), reads=["pM"], writes=["eGi"])
            sc.op("dve", (lambda e: e.scalar_tensor_tensor(out=qtT[:], in0=gqk[s2][:, 0, :], scalar=0.125, in1=eG[:], op0=ALU.mult, op1=ALU.mult)), reads=["gqk%d" % s2, "eG"], writes=["qtT"])
            sc.op("dve", (lambda e: e.tensor_tensor(out=ktT[:], in0=gqk[s2][:, 1, :], in1=eGi[:], op=ALU.mult)), reads=["gqk%d" % s2, "eGi"], writes=["ktT"])
            sc.op("pe", (lambda e: e.transpose(out=pT[:, 7, 0:64], in_=ktT[:], identity=identb[0:64, 0:64])), reads=["ktT", "identb"], writes=["pT"])
            sc.op("dve", (lambda e: e.tensor_copy(out=kt_tm[:], in_=pT[:, 7, 0:64])), reads=["pT"], writes=["kt_tm"])
            sc.op("pe", (lambda e: e.matmul(pM[:, 256:384], lhsT=ktT[:], rhs=qtT[:], start=True, stop=True)), reads=["ktT", "qtT"], writes=["pM"])
            sc.op("dve", (lambda e: e.tensor_tensor(out=AT[:], in0=pM[:, 256:384], in1=trif[:], op=ALU.mult)), reads=["pM", "trif"], writes=["AT"])
            sc.op("pe", (lambda e: e.matmul(pM[:, 384:512], lhsT=AT[:], rhs=gv_tm[s2][:], start=True, stop=False)), reads=["AT", GV], writes=["pM"], signal=False, hold=True)
            sc.op("pe", (lambda e: e.matmul(pM[:, 384:512], lhsT=qtT[:], rhs=Sbf[:], start=False, stop=True)), reads=["qtT", "Sbf"], writes=["pM"])
            c0, c1 = 8 + (T % 2) * 2, 9 + (T % 2) * 2
            sc.op("act", (lambda e: e.activation(out=junk[:, 0:128], in_=pM[:, 384:512], func=AF.Square, accum_out=st1[:, c0:c0 + 1])), reads=["pM"], writes=["junk", "st1_%d" % c0])
            sc.op("act", (lambda e: e.activation(out=st1[:, c1:c1 + 1], in_=st1[:, c0:c0 + 1], func=AF.Ln, scale=1.0 / 128.0, bias=epsb[:, 0:1])), reads=["st1_%d" % c0, "epsb"], writes=["st1_%d" % c1])
            sc.op("act", (lambda e: e.activation(out=st1[:, c1:c1 + 1], in_=st1[:, c1:c1 + 1], func=AF.Exp, scale=-0.5)), reads=["st1_%d" % c1], writes=["st1_%d" % c1])
            sc.op("dve", (lambda e: e.scalar_tensor_tensor(out=ogt[:], in0=pM[:, 384:512], scalar=st1[:, c1:c1 + 1], in1=gngb[:], op0=ALU.mult, op1=ALU.mult)), reads=["pM", "st1_%d" % c1, "gngb"], writes=["ogt"])
            YT = "y_tm%d" % (g % 2)
            SGT = "sgt%d" % s2
            sc.op("dve", (lambda e: e.tensor_scalar(out=sgt[s2][:], in0=sgt[s2][:], scalar1=1.0, scalar2=None, op0=ALU.add)), reads=[SGT], writes=[SGT])
            sc.op("dve", (lambda e: e.reciprocal(out=sgt[s2][:], in_=sgt[s2][:])), reads=[SGT], writes=[SGT])
            sc.op("pool", (lambda e: e.tensor_tensor(out=sgr[s2][:], in0=sgr[s2][:], in1=sgt[s2][:], op=ALU.mult)), reads=[SGR, SGT], writes=[SGR])
            sc.op("pool", (lambda e: e.tensor_tensor(out=y_tm[g % 2][:, ti, 128:256], in0=ogt[:], in1=sgr[s2][:], op=ALU.mult)), reads=["ogt", SGR], writes=[YT + "g%d" % ti])
            sc.op("pe", (lambda e: e.matmul(pM[0:64, 0:128], lhsT=kt_tm[:], rhs=gv_tm[s2][:], start=True, stop=True)), reads=["kt_tm", GV], writes=["pM"])
            sc.op("dve", (lambda e: e.tensor_scalar(out=stmp[:], in0=pM[0:64, 0:128], scalar1=eG[:, 127:128], scalar2=None, op0=ALU.mult)), reads=["pM", "eG"], writes=["stmp"])
            sc.op("dve", (lambda e: e.scalar_tensor_tensor(out=Sst[:], in0=Sst[:], scalar=eG[:, 127:128], in1=stmp[:], op0=ALU.mult, op1=ALU.add)), reads=["Sst", "eG", "stmp"], writes=["Sst"])
            sc.op("pool", (lambda e: e.tensor_copy(out=Sbf[:], in_=Sst[:])), reads=["Sst"], writes=["Sbf"])

        def gate_tile(g, ti):
            T = g * 4 + ti
            own = T // 2
            qi = g % 2
            QXS = ["Qx%d_%d" % (h, qi) for h in range(2)]
            for h in range(2):
                sc.op("pe", (lambda e, h=h: e.matmul(pM[:, h * 32:(h + 1) * 32], lhsT=Qx[h][qi][0:64, ti * 128:(ti + 1) * 128], rhs=kms[h][:], start=True, stop=True)),
                      reads=[QXS[h], "kms%d" % h], writes=["pM"], signal=(h == 1))
            for h in range(2):
                sc.op("dve", (lambda e, h=h: e.tensor_tensor(out=gm2[:, h, :], in0=pM[:, h * 32:(h + 1) * 32], in1=addtab[:, own, :], op=ALU.add)), reads=["pM", "addtab"], writes=["gm%d" % h])
            for h in range(2):
                sc.op("dve", (lambda e, h=h: e.max(out=m82[:, h, :], in_=gm2[:, h, :])), reads=["gm%d" % h], writes=["m8_%d" % h])
            sc.op("dve", (lambda e: e.tensor_scalar(out=thr2[:], in0=m82[:, :, 3], scalar1=-1.0e29, scalar2=None, op0=ALU.max)), reads=["m8_0", "m8_1"], writes=["thr"])
            for h in range(2):
                sc.op("dve", (lambda e, h=h: e.tensor_scalar(out=Mpad2[h][:, 64:96], in0=gm2[:, h, :], scalar1=thr2[:, h:h + 1], scalar2=NEG, op0=ALU.is_lt, op1=ALU.mult)), reads=["gm%d" % h, "thr"], writes=["Mpad%d" % h])
            for h in range(2):
                sc.op("pe", (lambda e, h=h: e.transpose(out=pT[0:96, 6 + h, :], in_=Mpad2[h][:], identity=identb[:])), reads=["Mpad%d" % h, "identb"], writes=["pT"], signal=(h == 1))
            for h in range(2):
                sc.op("dve", (lambda e, h=h: e.tensor_copy(out=Qx[h][qi][64:96, ti * 128:(ti + 1) * 128], in_=pT[64:96, 6 + h, :])), reads=["pT"], writes=[QXS[h]])

        def attn_group(g, h, bg=None, iters_after=0, pending=None, after_pending=None):
            qi = g % 2
            QX = "Qx%d_%d" % (h, qi)
            nkt = 4 * g + 4

            def qk(kt):
                sp = kt % 2
                sc.op("pe", (lambda e, kt=kt, sp=sp: e.matmul(pS[sp][:], lhsT=Kx[h][0:96, kt * 128:(kt + 1) * 128], rhs=Qx[h][qi][0:96, :], start=True, stop=True)),
                      reads=["Kx%d_%d" % (h, kt // 4), "KxE%d" % h, QX], writes=["pS%d" % sp])

            def ex(kt):
                sp, pi = kt % 2, kt % 3
                PTK = "PT%d" % pi
                sc.op("act", (lambda e: e.activation(out=PT[pi][:], in_=pS[sp][:], func=AF.Exp, scale=0.125)), reads=["pS%d" % sp], writes=[PTK])
                j = kt - 4 * g
                if j >= 0:
                    sc.op("pool", (lambda e: e.tensor_tensor(out=PT[pi][:, 128 * j:128 * j + 128], in0=PT[pi][:, 128 * j:128 * j + 128], in1=trib[:], op=ALU.mult)), reads=[PTK, "trib"], writes=[PTK])
                    if j % 2 == 1:
                        sc.op("pool", (lambda e: e.memset(PT[pi][:, 128 * (j - 1):128 * j], 0.0)), writes=[PTK])

            def pv(kt):
                pi = kt % 3
                sc.op("pe", (lambda e: e.matmul(pO[0:65, :], lhsT=Vx[:, kt, h, 0:65], rhs=PT[pi][:], start=(kt == 0), stop=(kt == nkt - 1))),
                      reads=["Vx", "Vx_%d" % (kt // 4), "PT%d" % pi], writes=["pO"], signal=True)

            qk(0)
            for kt in range(nkt):
                if kt + 1 < nkt:
                    qk(kt + 1)
                if kt == 0 and pending:
                    sc.run_deferred(pending, len(pending))
                    if after_pending is not None:
                        after_pending()
                ex(kt)
                if kt >= 1:
                    pv(kt - 1)
                if bg:
                    sc.run_deferred(bg, 0, iters_left=nkt - kt + iters_after)
            pv(nkt - 1)
            sc.op("dve", (lambda e: e.tensor_copy(out=OT[:], in_=pO[0:65, :])), reads=["pO"], writes=["OT"])
            fin_steps = []
            sc.defer = fin_steps
            for ti in range(4):
                bank, c0, BK = (pb, 384, "pb") if ti % 2 == 0 else (pa, 400, "pa")
                sc.op("pe", (lambda e, ti=ti, bank=bank, c0=c0: e.transpose(out=bank[:, c0:c0 + 65], in_=OT[:, ti * 128:(ti + 1) * 128], identity=identf[0:65, 0:65])), reads=["OT", "identf"], writes=[BK])
                sc.op("dve", (lambda e, ti=ti, bank=bank, c0=c0: e.reciprocal(out=rcp2[:, ti % 2:ti % 2 + 1], in_=bank[:, c0 + 64:c0 + 65])), reads=[BK], writes=["rcp%d" % (ti % 2)])
                sc.op("dve", (lambda e, ti=ti, bank=bank, c0=c0: e.tensor_scalar(out=y_tm[g % 2][:, ti, h * 64:(h + 1) * 64], in0=bank[:, c0:c0 + 64], scalar1=rcp2[:, ti % 2:ti % 2 + 1], scalar2=None, op0=ALU.mult)), reads=[BK, "rcp%d" % (ti % 2)], writes=["y_tm%d" % (g % 2) + "m%d_%d" % (h, ti)])
            sc.defer = None
            return fin_steps

        oneb = sb("oneb", [128, 1], F32)
        sc.op("pool", lambda e: e.memset(oneb[:], 1.0), writes=["oneb"])

        cc_next = [0]

        def emit_cc(k):
            sc.dma("pool", "cc", (lambda e, k=k: e.collective_compute("AllGather", ALU.bypass, replica_groups=[[0, 1, 2, 3], [4, 5, 6, 7]],
                                                                        ins=[ctx["ybuf"][k * CH:(k + 1) * CH, :]], outs=[ctx["yall"][k * 4 * CH:(k + 1) * 4 * CH, :]])),
                   reads=["ych%d" % k, "ccchain"], writes=["yall", "ccchain"], inc=1)

        if fused:
            zt = sb("zt", [128, 256], BF16)
            sc.op("pool", lambda e: e.memset(zt[:], 0.0), writes=["zt"])
            sc.dma("sp", "yzh", lambda e: e.dma_start(out=ctx["ybuf"][0:HALO, :], in_=zt[:]), reads=["zt"], writes=["ych0"])
            for r0 in range(HALO + S1, NCH * CH, 128):
                sc.dma("sp", "yzt", (lambda e, r0=r0: e.dma_start(out=ctx["ybuf"][r0:r0 + 128, :], in_=zt[:])), reads=["zt"], writes=["ych%d" % (NCH - 1)])

        SKIP = os.environ.get("K_SKIP", "").split(",")
        def rec(fn, *a):
            lst = []
            sc.defer = lst
            fn(*a)
            sc.defer = None
            return lst

        def front(g):
            P = [rec(proj_tile, g, ti) for ti in range(4)]
            G = [rec(gla_chunk, g, ti) for ti in range(4)]
            gates = [rec(gate_tile, g, ti) for ti in range(4)] if "gate" not in SKIP else []
            ga = [st for l in gates[0::2] for st in l]
            gb = [st for l in gates[1::2] for st in l]
            return (P[0] + merge_steps(G[0], P[1]) + merge_steps(G[1], P[2]) + merge_steps(G[2], P[3]) + G[3] + ga + gb)

        def store_group(g):
            yt_tokens = ["y_tm%d" % (g % 2) + "g%d" % ti for ti in range(4)] + ["y_tm%d" % (g % 2) + "m%d_%d" % (h, ti) for h in range(2) for ti in range(4)]
            ychs = (["ych%d" % k for k in range((HALO + g * 512) // CH, (HALO + g * 512 + 511) // CH + 1)] if fused else ["y_out"])
            sc.dma("sp", "yst%d" % (g % 2), (lambda e, g=g: e.dma_start(out=y_out[g * 512:(g + 1) * 512, :].rearrange("(a p) c -> p a c", p=128), in_=y_tm[g % 2][:])), reads=yt_tokens, writes=ychs + ["ydone%d" % (g % 2)])
            if fused:
                while cc_next[0] < NCH and HALO + (g + 1) * 512 >= (cc_next[0] + 1) * CH:
                    emit_cc(cc_next[0])
                    cc_next[0] += 1
            for tk in yt_tokens:
                sc.last_w[tk] = sc.last_w["ydone%d" % (g % 2)]
                sc.readers[tk] = []

        wpf = []
        if fused:
            fgm = sb("fgm", [128, 8], F32)
            sc.dma("sp", "kfg", lambda e: e.dma_start(out=fgm[:], in_=ctx["ffn_g"]), writes=["fgm"])
            wsf = [sb("wsf%d" % i, [128, 1024], F32) for i in range(3)]
            wsb = [sb("wsb%d" % i, [128, 1024], BF16) for i in range(3)]
            jobs = [(ctx["w_out"][k * 128:(k + 1) * 128, :], ctx["wout_scr"][:, k, :], 1024, None) for k in range(8)]
            for k in range(8):
                for c0 in range(0, 2 * DFF, 1024):
                    wid = min(1024, 2 * DFF - c0)
                    jobs.append((ctx["w_up"][k * 128:(k + 1) * 128, c0:c0 + wid], ctx["wup_scr"][:, k, c0:c0 + wid], wid, k))
            for fc in range(NFC):
                jobs.append((ctx["w_dn"][fc * 128:(fc + 1) * 128, :], ctx["wdn_scr"][:, fc, :], 1024, None))
            sc.defer = wpf
            for j, (src, dst, wid, gk) in enumerate(jobs):
                q = j % 3
                sc.dma("sp", "wpi%d" % q, (lambda e, q=q, src=src, wid=wid: e.dma_start(out=wsf[q][:, 0:wid], in_=src)), writes=["wsf%d" % q])
                if gk is None:
                    sc.op("dve", (lambda e, q=q, wid=wid: e.tensor_copy(out=wsb[q][:, 0:wid], in_=wsf[q][:, 0:wid])), reads=["wsf%d" % q], writes=["wsb%d" % q])
                else:
                    sc.op("dve", (lambda e, q=q, wid=wid, gk=gk: e.tensor_scalar(out=wsb[q][:, 0:wid], in0=wsf[q][:, 0:wid], scalar1=fgm[:, gk:gk + 1], scalar2=None, op0=ALU.mult)),
                          reads=["wsf%d" % q, "fgm"], writes=["wsb%d" % q])
                sc.dma("sp", "wpo%d" % q, (lambda e, q=q, dst=dst, wid=wid: e.dma_start(out=dst, in_=wsb[q][:, 0:wid])), reads=["wsb%d" % q], writes=["wscr"])
            sc.defer = None
        wpf_per_group = 3 * (-(-len(wpf) // 3 // max(1, NG1 - 1)))

        pending = None
        pending_store = [None]
        first = front(0)
        sc.run_deferred(first, len(first))
        for g in range(NG1):
            bg = front(g + 1) if g + 1 < NG1 else []
            if wpf:
                take = wpf_per_group if g + 1 < NG1 else len(wpf)
                chunk, wpf[:] = wpf[:take], wpf[take:]
                bg = merge_steps(bg, chunk) if bg else chunk
            for h in range(2):
                fin = attn_group(g, h, bg, (1 - h) * (4 * g + 4), pending, pending_store[0])
                pending_store[0] = None
                pending = fin
                if h == 1:
                    pending_store[0] = (lambda g=g: store_group(g))
            sc.run_deferred(bg, len(bg))
        sc.run_deferred(pending, len(pending))
        pending_store[0]()
        if fused:
            while cc_next[0] < NCH:
                emit_cc(cc_next[0])
                cc_next[0] += 1
        else:
            sc.wait_all("sp", ["y_out"])
        with nc.Block() as block:
            sc.emit(block)
        if fused:
            ctx["pre_waits"] = sc.final_waits()
    return nc


CH = min(1024, NTOK2)
CPS = NTOK2 // CH
NCH = (HALO + S1 + CH - 1) // CH


def build_fused():
    nc = bass.Bass("TRN2", target_bir_lowering=False)
    es_sem = contextlib.ExitStack()
    with es_sem:
        ctx = {"nc": nc, "es_sem": es_sem}
        ctx["xpad"] = nc.dram_tensor("xpad", [HALO + S1, D], F32, kind="ExternalInput").ap()
        ctx["ybuf"] = nc.dram_tensor("ybuf", [NCH * CH, 256], BF16, kind="Internal").ap()
        ctx["yall"] = nc.dram_tensor("yall", [NCH * 4 * CH, 256], BF16, kind="Internal").ap()
        ctx["w_out"] = nc.dram_tensor("w_out", [D, D], F32, kind="ExternalInput").ap()
        ctx["w_up"] = nc.dram_tensor("w_up", [D, 2 * DFF], F32, kind="ExternalInput").ap()
        ctx["w_dn"] = nc.dram_tensor("w_dn", [DFF, D], F32, kind="ExternalInput").ap()
        ctx["ffn_g"] = nc.dram_tensor("ffn_g", [128, 8], F32, kind="ExternalInput").ap()
        ctx["wout_scr"] = nc.dram_tensor("wout_scr", [128, 8, D], BF16, kind="Internal").ap()
        ctx["wup_scr"] = nc.dram_tensor("wup_scr", [128, 8, 2 * DFF], BF16, kind="Internal").ap()
        ctx["wdn_scr"] = nc.dram_tensor("wdn_scr", [128, NFC, D], BF16, kind="Internal").ap()
        build_mixer(ctx)
        build_ffn(ctx)
    return nc


def mixer_inputs(x, attn_norm_g, w_in, w_gate_up, b_gate, gla_norm_g):
    o_mq, o_mk, o_mv, o_gq, o_gk, o_gv, o_gr, o_gg = 0, 512, 1024, 1536, 1792, 2048, 2560, 3072
    pos = np.arange(S1, dtype=np.float32)
    inv_freq = (1.0 / (10000.0 ** (np.arange(0, 64, 2, dtype=np.float32) / np.float32(64)))).astype(np.float32)
    ang = (pos[:, None] * inv_freq[None, :]).astype(np.float32)
    cos = np.cos(ang).astype(np.float32)
    sin = np.sin(ang).astype(np.float32)
    lay = lambda t: np.ascontiguousarray(np.tile(t, (1, 4)).reshape(NT1, 128, 128).transpose(1, 0, 2))
    cos4, sin4 = lay(cos), lay(sin)
    ehot = (np.arange(32)[:, None] == (np.arange(S1)[None, :] // 256)).astype(np.float32).astype(ml_dtypes.bfloat16)
    n = np.arange(32)
    addtab = np.where(n[None, :] < n[:, None], 0.0, np.where(n[None, :] == n[:, None], BIG, -BIG)).astype(np.float32).reshape(1, 32 * 32)
    tri = (np.arange(128)[None, :] >= np.arange(128)[:, None]).astype(np.float32)
    ident = np.eye(128, dtype=np.float32)
    shared = {
        "attn_g": np.ascontiguousarray(attn_norm_g[0].reshape(8, 128).T),
        "cos4": cos4, "sin4": sin4, "ehot": ehot, "addtab": addtab,
        "tri_f": tri, "tri_b": tri.astype(ml_dtypes.bfloat16),
        "ident_b": ident.astype(ml_dtypes.bfloat16), "ident_f": ident,
    }
    W = w_in[0]
    maps = []
    for core in range(8):
        b, j = core // 4, core % 4
        h0, h1 = 2 * j, 2 * j + 1
        cols = np.concatenate([
            np.arange(o_mq + h0 * 64, o_mq + h0 * 64 + 64), np.arange(o_mq + h1 * 64, o_mq + h1 * 64 + 64),
            np.arange(o_mk + h0 * 64, o_mk + h0 * 64 + 64), np.arange(o_mk + h1 * 64, o_mk + h1 * 64 + 64),
            np.arange(o_gq + j * 64, o_gq + j * 64 + 64), np.arange(o_gk + j * 64, o_gk + j * 64 + 64),
            np.arange(o_gg, o_gg + 16),
            np.arange(o_mv + h0 * 64, o_mv + h0 * 64 + 64), np.arange(o_mv + h1 * 64, o_mv + h1 * 64 + 64),
            np.arange(o_gv + j * 128, o_gv + j * 128 + 128), np.arange(o_gr + j * 128, o_gr + j * 128 + 128),
        ])
        m = dict(shared)
        m["x"] = np.ascontiguousarray(x[b, :S1])
        m["w"] = np.ascontiguousarray(W[:, cols])
        m["wg"] = np.ascontiguousarray(w_gate_up[0][:, j * 64:(j + 1) * 64])
        m["bg"] = np.ascontiguousarray(b_gate[0][j * 64:(j + 1) * 64].reshape(1, 64))
        m["gng"] = np.ascontiguousarray(gla_norm_g[0][j].reshape(1, 128))
        maps.append(m)
    return maps


_NC_CACHE = {}


def fused_inputs(x, attn_norm_g, w_in, w_gate_up, b_gate, gla_norm_g, w_out,
                 ffn_norm_g, w_ffn_up, conv_w, conv_b, w_ffn_down, final_norm_g):
    m1 = mixer_inputs(x, attn_norm_g, w_in, w_gate_up, b_gate, gla_norm_g)
    shared2 = {
        "w_out": np.ascontiguousarray(w_out[0]),
        "w_up": np.ascontiguousarray(w_ffn_up[0]),
        "w_dn": np.ascontiguousarray(w_ffn_down[0]),
        "ffn_g": np.ascontiguousarray(ffn_norm_g[0].reshape(8, 128).T),
        "conv_w": np.ascontiguousarray(conv_w[0].reshape(3, 2 * NFC, 128).transpose(2, 0, 1)),
        "conv_b": np.ascontiguousarray(conv_b[0].reshape(2 * NFC, 128).T),
        "final_g": np.ascontiguousarray(final_norm_g.reshape(1, D)),
    }
    xpads = []
    for b in range(B):
        xp = np.zeros((HALO + S1, D), np.float32)
        xp[HALO:] = x[b, :S1]
        xpads.append(xp)
    maps = []
    for core in range(8):
        m = dict(m1[core])
        del m["x"]
        m.update(shared2)
        m["xpad"] = xpads[core // 4]
        c = core % 4
        m["xown"] = np.ascontiguousarray(xpads[core // 4][c * NTOK2:c * NTOK2 + HALO + NTOK2])
        maps.append(m)
    return maps


def kernel_fused(x, attn_norm_g, w_in, w_gate_up, b_gate, gla_norm_g, w_out,
                 ffn_norm_g, w_ffn_up, conv_w, conv_b, w_ffn_down, final_norm_g):
    if "fused" not in _NC_CACHE:
        _NC_CACHE["fused"] = build_fused()
    maps = fused_inputs(x, attn_norm_g, w_in, w_gate_up, b_gate, gla_norm_g, w_out,
                        ffn_norm_g, w_ffn_up, conv_w, conv_b, w_ffn_down, final_norm_g)
    res = run_bass_kernel_spmd(_NC_CACHE["fused"], maps, core_ids=list(range(8)))
    out = np.stack([np.concatenate([np.asarray(res.results[b * 4 + c]["out"]) for c in range(4)], axis=0) for b in range(B)])
    return out.astype(np.float32)


def kernel(x, attn_norm_g, w_in, w_gate_up, b_gate, gla_norm_g, w_out,
           ffn_norm_g, w_ffn_up, conv_w, conv_b, w_ffn_down, final_norm_g):
    args = [np.asarray(a, np.float32) for a in (x, attn_norm_g, w_in, w_gate_up, b_gate, gla_norm_g, w_out,
                                                 ffn_norm_g, w_ffn_up, conv_w, conv_b, w_ffn_down, final_norm_g)]
    return kernel_fused(*args)


def kernel_unfused(x, attn_norm_g, w_in, w_gate_up, b_gate, gla_norm_g, w_out,
                   ffn_norm_g, w_ffn_up, conv_w, conv_b, w_ffn_down, final_norm_g):
    x = np.asarray(x, np.float32)
    args = [np.asarray(a, np.float32) for a in (attn_norm_g, w_in, w_gate_up, b_gate, gla_norm_g, w_out,
                                                 ffn_norm_g, w_ffn_up, conv_w, conv_b, w_ffn_down, final_norm_g)]
    (attn_norm_g, w_in, w_gate_up, b_gate, gla_norm_g, w_out,
     ffn_norm_g, w_ffn_up, conv_w, conv_b, w_ffn_down, final_norm_g) = args
    if "mix" not in _NC_CACHE:
        _NC_CACHE["mix"] = build_mixer()
    maps = mixer_inputs(x, attn_norm_g, w_in, w_gate_up, b_gate, gla_norm_g)
    res = run_bass_kernel_spmd(_NC_CACHE["mix"], maps, core_ids=list(range(8)))
    yT_full = np.zeros((B, D, S), ml_dtypes.bfloat16)
    for core in range(8):
        b, j = core // 4, core % 4
        yc = np.asarray(res.results[core]["y"])
        yT_full[b, 128 * j:128 * j + 128, :] = yc[:, 0:128].T
        yT_full[b, 512 + 128 * j:512 + 128 * j + 128, :] = yc[:, 128:256].T
    if "ffn" not in _NC_CACHE:
        _NC_CACHE["ffn"] = build_ffn()
    maps2 = ffn_inputs(x, yT_full, w_out, w_ffn_up, w_ffn_down, ffn_norm_g, conv_w, conv_b, final_norm_g)
    res2 = run_bass_kernel_spmd(_NC_CACHE["ffn"], maps2, core_ids=list(range(8)))
    out = np.stack([np.concatenate([np.asarray(res2.results[b * 4 + c]["out"]) for c in range(4)], axis=0) for b in range(B)])
    return out.astype(np.float32)
```
